# Optimizing a Trainium2 kernel written in Bass

```python
import math
import jax, jax.numpy as jnp
from jax import lax
import numpy as np

D_MODEL = 1024
BATCH = 8
SEQ = 4096
DEPTH = 2

N_META = 16
HEAD_DIM = 64
D_MIX = D_MODEL
GROUP_W = D_MIX // 4
N_HEADS = GROUP_W // HEAD_DIM
D_FF = 4 * D_MODEL
RET_CHUNK = 64
GLA_CHUNK = 16
ROPE_BASE = 10000.0
RWKV_DECAY_RANK = 64
RWKV_A_RANK = 64
RWKV_V_RANK = 32
RWKV_G_RANK = 160
S5_CH = 16
S5_GROUPS = GROUP_W // S5_CH
S5_STATE = 64
NORM_EPS = 1e-6
RWKV_LN_EPS = 64e-5
RET_COLS = 4 * GROUP_W
HGRN_COLS = 4 * GROUP_W
RWKV_SPLITS = (GROUP_W, GROUP_W, GROUP_W, RWKV_DECAY_RANK, RWKV_A_RANK, RWKV_G_RANK)
RWKV_COLS = sum(RWKV_SPLITS)
S5_COLS = GROUP_W
N_IN = RET_COLS + HGRN_COLS + RWKV_COLS + S5_COLS

kernel_name = 'hybrid_parallel_heads_ret_hgrn2_rwkv7_s5'


def split_cols(z, widths):
    idx = [int(i) for i in np.cumsum(widths)[:-1]]
    return jnp.split(z, idx, axis=-1)


def heads(t):
    return t.reshape(t.shape[:-1] + (t.shape[-1] // HEAD_DIM, HEAD_DIM))


def rmsnorm(x, g):
    xf = x.astype(jnp.float32)
    y = xf * lax.rsqrt(jnp.mean(xf * xf, axis=-1, keepdims=True) + NORM_EPS)
    return (y * g.astype(jnp.float32)).astype(x.dtype)


def rms_heads(o):
    return o * lax.rsqrt(jnp.mean(o * o, axis=-1, keepdims=True) + NORM_EPS)


def rotary(t, pos):
    half = HEAD_DIM // 2
    inv_freq = ROPE_BASE ** (-jnp.arange(half, dtype=jnp.float32) / half)
    ang = pos.astype(jnp.float32)[:, None] * inv_freq[None, :]
    cos, sin = jnp.cos(ang)[None, :, None, :], jnp.sin(ang)[None, :, None, :]
    t1, t2 = t[..., :half], t[..., half:]
    return jnp.concatenate([t1 * cos - t2 * sin, t1 * sin + t2 * cos], axis=-1)


def over_segments(chunk_fn, arrays, state0, chunk):
    out_meta, state = chunk_fn(*[a[:, :N_META] for a in arrays], state0, N_META)
    out_real, _ = chunk_fn(*[a[:, N_META:] for a in arrays], state, chunk)
    return jnp.concatenate([out_meta, out_real], axis=1)


def retention_chunks(q, k, v, state, chunk):
    B_, T, H, d = q.shape
    n = T // chunk
    log_g = jnp.log1p(-jnp.exp2(-5.0 - jnp.arange(H, dtype=jnp.float32)))
    qc, kc, vc = (a.reshape(B_, n, chunk, H, d) for a in (q, k, v))
    t = jnp.arange(chunk, dtype=jnp.float32)
    rel = t[:, None] - t[None, :]
    causal = rel >= 0
    dmask = jnp.where(causal[None], jnp.exp(jnp.where(causal, rel, 0.0)[None] * log_g[:, None, None]), 0.0)
    scores = jnp.einsum('bnthd,bnshd->bnhts', qc, kc) * dmask
    intra = jnp.einsum('bnhts,bnshd->bnthd', scores, vc)
    k_dec = jnp.exp((chunk - 1.0 - t)[:, None] * log_g[None, :])
    q_dec = jnp.exp((t + 1.0)[:, None] * log_g[None, :])
    kv = jnp.einsum('bnshd,bnshe->nbhde', kc * k_dec[:, :, None], vc)
    q_in = jnp.moveaxis(qc * q_dec[:, :, None], 1, 0)
    g_chunk = jnp.exp(chunk * log_g)[None, :, None, None]

    def step(S, inp):
        q_n, kv_n = inp
        return g_chunk * S + kv_n, jnp.einsum('bthd,bhde->bthe', q_n, S)

    state, inter = lax.scan(step, state, (q_in, kv))
    out = intra + jnp.moveaxis(inter, 0, 1)
    return out.reshape(B_, T, H, d), state


def gla_chunks(q, k, v, log_f, state, chunk):
    B_, T, H, dk = q.shape
    n = T // chunk
    rs = lambda a: a.reshape(B_, n, chunk, H, a.shape[-1])
    qc, kc, vc = rs(q), rs(k), rs(v)
    b = jnp.cumsum(rs(log_f), axis=2)
    causal = jnp.tril(jnp.ones((chunk, chunk), dtype=bool))[None, None, :, :, None, None]
    diff = b[:, :, :, None] - b[:, :, None, :]
    dec = jnp.exp(jnp.where(causal, diff, -jnp.inf))
    scores = jnp.sum(qc[:, :, :, None] * kc[:, :, None] * dec, axis=-1)
    intra = jnp.einsum('bntsh,bnshe->bnthe', scores, vc)
    b_last = b[:, :, -1:]
    kv = jnp.einsum('bnshd,bnshe->nbhde', kc * jnp.exp(b_last - b), vc)
    q_in = jnp.moveaxis(qc * jnp.exp(b), 1, 0)
    f_chunk = jnp.moveaxis(jnp.exp(b_last[:, :, 0]), 1, 0)[..., None]

    def step(S, inp):
        q_n, kv_n, f_n = inp
        return f_n * S + kv_n, jnp.einsum('bthd,bhde->bthe', q_n, S)

    state, inter = lax.scan(step, state, (q_in, kv, f_chunk))
    out = intra + jnp.moveaxis(inter, 0, 1)
    return out.reshape(B_, T, H, v.shape[-1]), state


def rwkv7_scan(r, w, k, v, kk, a):
    B_, L, H, d = r.shape
    S0 = jnp.zeros((B_, H, d, d), jnp.float32)

    def step(S, inp):
        r_t, w_t, k_t, v_t, kk_t, a_t = inp
        sa = jnp.einsum('bhvk,bhk->bhv', S, -kk_t)
        S = S * w_t[:, :, None, :] + sa[..., None] * (kk_t * a_t)[:, :, None, :] + v_t[..., None] * k_t[:, :, None, :]
        return S, jnp.einsum('bhvk,bhk->bhv', S, r_t)

    _, o = lax.scan(step, S0, tuple(jnp.moveaxis(t, 1, 0) for t in (r, w, k, v, kk, a)))
    return jnp.moveaxis(o, 0, 1)


def complex_affine_combine(e1, e2):
    a1r, a1i, b1r, b1i = e1
    a2r, a2i, b2r, b2i = e2
    return (a2r * a1r - a2i * a1i,
            a2r * a1i + a2i * a1r,
            a2r * b1r - a2i * b1i + b2r,
            a2r * b1i + a2i * b1r + b2i)


def retention_mixer(z, pos):
    f32 = jnp.float32
    q, k, v, g = split_cols(z.astype(f32), (GROUP_W,) * 4)
    q = rotary(heads(q), pos)
    k = rotary(heads(k), pos) * HEAD_DIM ** -0.5
    s0 = jnp.zeros((z.shape[0], N_HEADS, HEAD_DIM, HEAD_DIM), f32)
    o = over_segments(retention_chunks, (q, k, heads(v)), s0, RET_CHUNK)
    o = rms_heads(o).reshape(z.shape[:-1] + (GROUP_W,))
    return (o * jax.nn.silu(g)).astype(z.dtype)


def hgrn2_mixer(z, lower_bound, norm_g):
    f32 = jnp.float32
    q, f, i, g = split_cols(z.astype(f32), (GROUP_W,) * 4)
    forget = lower_bound.astype(f32) + (1.0 - lower_bound.astype(f32)) * jax.nn.sigmoid(f)
    k = 1.0 - forget
    q = jax.nn.silu(q) * HEAD_DIM ** -0.5
    s0 = jnp.zeros((z.shape[0], N_HEADS, HEAD_DIM, HEAD_DIM), f32)
    o = over_segments(gla_chunks, (heads(q), heads(k), heads(i), heads(jnp.log(forget))), s0, GLA_CHUNK)
    o = (rms_heads(o) * norm_g.astype(f32)).reshape(z.shape[:-1] + (GROUP_W,))
    return (o * jax.nn.silu(g)).astype(z.dtype)


def rwkv7_mixer(z, v_first, vmix, mu, w0, w_up, a0, a_up, g_up, k_k, k_a, r_k, ln_w, ln_b):
    f32 = jnp.float32
    zf = z.astype(f32)
    prev = jnp.pad(zf, ((0, 0), (1, 0), (0, 0)))[:, :-1]
    zf = zf + (prev - zf) * mu.astype(f32)
    r, k, v, wl, al, gl = split_cols(zf, RWKV_SPLITS)
    w_log = -jax.nn.softplus(-(w0 + jnp.tanh(wl) @ w_up)) - 0.5
    decay = jnp.exp(-jnp.exp(w_log.astype(f32)))
    a = jax.nn.sigmoid(a0 + al @ a_up).astype(f32)
    g = (jax.nn.sigmoid(gl) @ g_up).astype(f32)
    if vmix is None:
        v_first = v
    else:
        v0, v_down, v_up = vmix
        v = v + (v_first - v) * jax.nn.sigmoid(v0 + (v @ v_down) @ v_up).astype(f32)
    kk = heads(k * k_k.astype(f32))
    kk = kk / jnp.maximum(jnp.sqrt(jnp.sum(kk * kk, axis=-1, keepdims=True)), 1e-12)
    k = k * (1.0 + (a - 1.0) * k_a.astype(f32))
    rh, kh, vh = heads(r), heads(k), heads(v)
    o = rwkv7_scan(rh, heads(decay), kh, vh, kk, heads(a))
    mean = jnp.mean(o, axis=-1, keepdims=True)
    var = jnp.mean(jnp.square(o - mean), axis=-1, keepdims=True)
    o = ((o - mean) * lax.rsqrt(var + RWKV_LN_EPS)).reshape(z.shape[:-1] + (GROUP_W,))
    o = heads(o * ln_w.astype(f32) + ln_b.astype(f32))
    o = o + jnp.sum(rh * kh * r_k.astype(f32), axis=-1, keepdims=True) * vh
    o = o.reshape(z.shape[:-1] + (GROUP_W,)) * g
    return o.astype(z.dtype), v_first


def s5_mixer(z, a_re, a_im, log_dt, b_re, b_im, c_re, c_im, d_skip, glu_w, glu_b):
    f32 = jnp.float32
    B_, L, _ = z.shape
    u = z.astype(f32)
    ug = u.reshape(B_, L, S5_GROUPS, S5_CH)
    lam_re, lam_im = a_re.astype(f32), a_im.astype(f32)
    dt = jnp.exp(log_dt.astype(f32))[:, None]
    mag, ph = jnp.exp(lam_re * dt), lam_im * dt
    ab_re, ab_im = mag * jnp.cos(ph), mag * jnp.sin(ph)
    den = lam_re * lam_re + lam_im * lam_im
    nr, ni = ab_re - 1.0, ab_im
    zc_re = (nr * lam_re + ni * lam_im) / den
    zc_im = (ni * lam_re - nr * lam_im) / den
    b_re, b_im = b_re.astype(f32), b_im.astype(f32)
    bb_re = zc_re[..., None] * b_re - zc_im[..., None] * b_im
    bb_im = zc_re[..., None] * b_im + zc_im[..., None] * b_re
    bu_re = jnp.einsum('blgc,gpc->blgp', ug, bb_re)
    bu_im = jnp.einsum('blgc,gpc->blgp', ug, bb_im)
    elems = (jnp.broadcast_to(ab_re, bu_re.shape), jnp.broadcast_to(ab_im, bu_re.shape), bu_re, bu_im)
    _, _, x_re, x_im = lax.associative_scan(complex_affine_combine, elems, axis=1)
    y = jnp.einsum('blgp,gcp->blgc', x_re, c_re.astype(f32)) - jnp.einsum('blgp,gcp->blgc', x_im, c_im.astype(f32))
    y = jax.nn.gelu(y.reshape(B_, L, GROUP_W) + d_skip.astype(f32) * u)
    return (y * jax.nn.sigmoid(y @ glu_w.astype(f32) + glu_b.astype(f32))).astype(z.dtype)


def setup_inputs(seed: int = 0) -> dict:
    key = jax.random.key(seed)
    ks = iter(jax.random.split(key, 48))
    f32 = jnp.float32
    nrm = lambda shape, scale: jax.random.normal(next(ks), shape, f32) * scale
    uni = lambda shape, lo, hi: jax.random.uniform(next(ks), shape, f32, lo, hi)
    G, P = S5_GROUPS, S5_STATE
    return {
        'x': nrm((BATCH, SEQ, D_MODEL), 1.0),
        'meta_tokens': nrm((N_META, D_MODEL), 1.0),
        'norm_mix_g': 1.0 + nrm((DEPTH, D_MODEL), 0.02),
        'w_in': nrm((DEPTH, D_MODEL, N_IN), D_MODEL ** -0.5),
        'hgrn_lb_logits': nrm((DEPTH, GROUP_W), 0.1),
        'hgrn_norm_g': 1.0 + nrm((DEPTH, HEAD_DIM), 0.02),
        'rwkv_mu': uni((DEPTH, RWKV_COLS), 0.0, 1.0),
        'rwkv_w0': uni((DEPTH, GROUP_W), -6.0, -1.0),
        'rwkv_w_up': nrm((DEPTH, RWKV_DECAY_RANK, GROUP_W), RWKV_DECAY_RANK ** -0.5),
        'rwkv_a0': nrm((DEPTH, GROUP_W), 0.1),
        'rwkv_a_up': nrm((DEPTH, RWKV_A_RANK, GROUP_W), RWKV_A_RANK ** -0.5),
        'rwkv_g_up': nrm((DEPTH, RWKV_G_RANK, GROUP_W), RWKV_G_RANK ** -0.5),
        'rwkv_k_k': 0.85 + nrm((DEPTH, GROUP_W), 0.02),
        'rwkv_k_a': 1.0 + nrm((DEPTH, GROUP_W), 0.02),
        'rwkv_r_k': nrm((DEPTH, N_HEADS, HEAD_DIM), 0.1),
        'rwkv_ln_w': 1.0 + nrm((DEPTH, GROUP_W), 0.02),
        'rwkv_ln_b': nrm((DEPTH, GROUP_W), 0.02),
        'rwkv_v0': nrm((DEPTH - 1, GROUP_W), 0.1),
        'rwkv_v_down': nrm((DEPTH - 1, GROUP_W, RWKV_V_RANK), GROUP_W ** -0.5),
        'rwkv_v_up': nrm((DEPTH - 1, RWKV_V_RANK, GROUP_W), RWKV_V_RANK ** -0.5),
        's5_a_re': -0.5 + nrm((DEPTH, G, P), 0.01),
        's5_a_im': jnp.pi * jnp.arange(P, dtype=f32)[None, None, :] + nrm((DEPTH, G, P), 0.01),
        's5_log_dt': uni((DEPTH, G), math.log(1e-3), math.log(1e-1)),
        's5_b_re': nrm((DEPTH, G, P, S5_CH), (2 * S5_CH) ** -0.5),
        's5_b_im': nrm((DEPTH, G, P, S5_CH), (2 * S5_CH) ** -0.5),
        's5_c_re': nrm((DEPTH, G, S5_CH, P), P ** -0.5),
        's5_c_im': nrm((DEPTH, G, S5_CH, P), P ** -0.5),
        's5_d': nrm((DEPTH, GROUP_W), 1.0),
        's5_glu_w': nrm((DEPTH, GROUP_W, GROUP_W), GROUP_W ** -0.5),
        's5_glu_b': nrm((DEPTH, GROUP_W), 0.02),
        'w_out': nrm((DEPTH, D_MIX, D_MODEL), D_MIX ** -0.5),
        'norm_ffn_g': 1.0 + nrm((DEPTH, D_MODEL), 0.02),
        'w_ffn_up': nrm((DEPTH, D_MODEL, D_FF), D_MODEL ** -0.5),
        'w_ffn_down': nrm((DEPTH, D_FF, D_MODEL), D_FF ** -0.5),
        'norm_f_g': 1.0 + nrm((D_MODEL,), 0.02),
    }


def reference(x, meta_tokens, norm_mix_g, w_in, hgrn_lb_logits, hgrn_norm_g, rwkv_mu, rwkv_w0, rwkv_w_up,
              rwkv_a0, rwkv_a_up, rwkv_g_up, rwkv_k_k, rwkv_k_a, rwkv_r_k, rwkv_ln_w, rwkv_ln_b, rwkv_v0,
              rwkv_v_down, rwkv_v_up, s5_a_re, s5_a_im, s5_log_dt, s5_b_re, s5_b_im, s5_c_re, s5_c_im, s5_d,
              s5_glu_w, s5_glu_b, w_out, norm_ffn_g, w_ffn_up, w_ffn_down, norm_f_g):
    B_ = x.shape[0]
    meta = jnp.broadcast_to(meta_tokens.astype(x.dtype)[None], (B_, N_META, D_MODEL))
    h = jnp.concatenate([meta, x], axis=1)
    pos = jnp.arange(h.shape[1])
    p = jax.nn.softmax(hgrn_lb_logits.astype(jnp.float32), axis=0)
    lower_bounds = jnp.cumsum(p, axis=0) - p[0]
    v_first = None
    for l in range(DEPTH):
        xn = rmsnorm(h, norm_mix_g[l])
        z_ret, z_hgrn, z_rwkv, z_s5 = split_cols(xn @ w_in[l], (RET_COLS, HGRN_COLS, RWKV_COLS, S5_COLS))
        o_ret = retention_mixer(z_ret, pos)
        o_hgrn = hgrn2_mixer(z_hgrn, lower_bounds[l], hgrn_norm_g[l])
        vmix = None if l == 0 else (rwkv_v0[l - 1], rwkv_v_down[l - 1], rwkv_v_up[l - 1])
        o_rwkv, v_first = rwkv7_mixer(z_rwkv, v_first, vmix, rwkv_mu[l], rwkv_w0[l], rwkv_w_up[l], rwkv_a0[l],
                                      rwkv_a_up[l], rwkv_g_up[l], rwkv_k_k[l], rwkv_k_a[l], rwkv_r_k[l],
                                      rwkv_ln_w[l], rwkv_ln_b[l])
        o_s5 = s5_mixer(z_s5, s5_a_re[l], s5_a_im[l], s5_log_dt[l], s5_b_re[l], s5_b_im[l], s5_c_re[l],
                        s5_c_im[l], s5_d[l], s5_glu_w[l], s5_glu_b[l])
        h = h + jnp.concatenate([o_ret, o_hgrn, o_rwkv, o_s5], axis=-1) @ w_out[l]
        hn = rmsnorm(h, norm_ffn_g[l])
        h = h + jnp.square(jax.nn.relu(hn @ w_ffn_up[l])) @ w_ffn_down[l]
    return rmsnorm(h, norm_f_g)[:, N_META:]
```

```python
import numpy as np
from contextlib import ExitStack
import concourse.bass as bass
import concourse.mybir as mybir
from concourse.bass_utils import run_bass_kernel_spmd

F32 = mybir.dt.float32
BF16 = mybir.dt.bfloat16
AF = mybir.ActivationFunctionType
ALU = mybir.AluOpType
AX = mybir.AxisListType

D = 1024
NMETA = 16
SEQ = 4096
NIN = 3360
DFF = 4096
EPS = 1e-6


class Buf:
    __slots__ = ("name", "last_w", "reads")

    def __init__(self, name=""):
        self.name = name
        self.last_w = None
        self.reads = []


class T:
    def __init__(self, ap, name="", buf=None, semkey=None):
        self.ap = ap
        self.b = buf if buf is not None else Buf(name)
        self.semkey = semkey

    def __getitem__(self, k):
        return self.ap[k]


class Sched:
    ENG = ("pe", "act", "dve", "pool", "sp")

    def __init__(self, nc, es):
        self.nc = nc
        self.es = es
        self.q = {e: [] for e in self.ENG}
        self.cnt = {e: 0 for e in self.ENG}
        self.seen = {e: {} for e in self.ENG}
        self.dma_tot = {}
        self.nt = 0
        self.capture = None
        self.eng_free = {e: 0.0 for e in self.ENG}
        self.buf_ready = {}

    def _model(self, eng, reads, writes, cost, lat):
        t = self.eng_free[eng]
        for x in list(reads) + list(writes):
            t = max(t, self.buf_ready.get(id(x.b), 0.0))
        return t

    def _commit(self, eng, reads, writes, cost, lat, t):
        self.eng_free[eng] = t + cost
        for x in writes:
            self.buf_ready[id(x.b)] = t + cost + lat
        for x in reads:
            self.buf_ready[id(x.b)] = max(self.buf_ready.get(id(x.b), 0.0), t + cost)

    def run_threads(self, gens):
        lists = []
        for g in gens:
            self.capture = []
            for _ in g:
                pass
            lists.append(self.capture)
            self.capture = None
        pos = [0] * len(lists)
        while True:
            best = None
            for i, L in enumerate(lists):
                if pos[i] >= len(L):
                    continue
                kind, eng, fn, reads, writes, cost, lat = L[pos[i]]
                t = self._model(eng, reads, writes, cost, lat)
                if best is None or t < best[0] - 1e-9:
                    best = (t, i)
            if best is None:
                break
            t, i = best
            kind, eng, fn, reads, writes, cost, lat = lists[i][pos[i]]
            pos[i] += 1
            if kind == "op":
                self.op(eng, fn, reads, writes, cost=cost, _t=t)
            else:
                self.dma(eng, fn, reads, writes, _t=t)

    def sb(self, shape, dt=F32, name=None):
        self.nt += 1
        return self.es.enter_context(self.nc.sbuf_tensor(name or f"sb{self.nt}", list(shape), dt))

    def ps(self, shape, dt=F32, name=None):
        self.nt += 1
        return self.es.enter_context(self.nc.psum_tensor(name or f"ps{self.nt}", list(shape), dt))

    def _deps(self, eng, reads, writes):
        evs = []
        for t in reads:
            b = t.b
            if b.last_w is not None:
                evs.append(b.last_w)
        for t in writes:
            b = t.b
            if b.last_w is not None:
                evs.append(b.last_w)
            evs.extend(b.reads)
        waits = {}
        seen = self.seen[eng]
        for (k, v) in evs:
            if k == "pe" and eng == "pe":
                continue
            if seen.get(k, 0) >= v:
                continue
            if waits.get(k, 0) < v:
                waits[k] = v
        for k, v in waits.items():
            seen[k] = v
        return list(waits.items())

    def _update(self, ev, reads, writes):
        wb = [t.b for t in writes]
        for b in wb:
            b.last_w = ev
            b.reads = []
        for t in reads:
            b = t.b
            if b not in wb:
                b.reads.append(ev)
                if len(b.reads) > 32:
                    d = {}
                    for (k, v) in b.reads:
                        if d.get(k, 0) < v:
                            d[k] = v
                    b.reads = list(d.items())

    def op(self, eng, fn, reads=(), writes=(), cost=0.4, _t=None):
        if self.capture is not None:
            self.capture.append(("op", eng, fn, reads, writes, cost, 0.15))
            return
        t = self._model(eng, reads, writes, cost, 0.15) if _t is None else _t
        self._commit(eng, reads, writes, cost, 0.15, t)
        waits = self._deps(eng, reads, writes)
        self.cnt[eng] += 1
        ev = (eng, self.cnt[eng])
        self.q[eng].append(("op", fn, waits, None))
        self._update(ev, reads, writes)

    def dma(self, eng, fns, reads, writes, _t=None):
        if self.capture is not None:
            self.capture.append(("dma", eng, fns, reads, writes, 0.1, 2.5))
            return
        t = self._model(eng, reads, writes, 0.1, 2.5) if _t is None else _t
        self._commit(eng, reads, writes, 0.1, 2.5, t)
        if not isinstance(fns, (list, tuple)):
            fns = [fns]
        waits = self._deps(eng, reads, writes)
        key = ("dma", getattr(writes[0], "semkey", None) or id(writes[0].b))
        self._keep = getattr(self, "_keep", [])
        self._keep.append(writes[0].b)
        self.dma_tot[key] = self.dma_tot.get(key, 0) + 16 * len(fns)
        ev = (key, self.dma_tot[key])
        first = True
        for fn in fns:
            self.q[eng].append(("dma", fn, waits if first else [], key))
            first = False
        self._update(ev, reads, writes)

    def barrier(self):
        for e in self.ENG:
            waits = []
            seen = self.seen[e]
            for k in self.ENG:
                if k != e and self.cnt[k] > seen.get(k, 0):
                    waits.append((k, self.cnt[k]))
                    seen[k] = self.cnt[k]
            for k, v in self.dma_tot.items():
                if v > seen.get(k, 0):
                    waits.append((k, v))
                    seen[k] = v
            if waits:
                self.q[e].append(("wait", None, waits, None))

    def emit(self):
        nc = self.nc
        sems = {}
        with ExitStack() as es2:
            for e in self.ENG:
                sems[e] = es2.enter_context(nc.semaphore(f"sem_{e}"))
            for i, k in enumerate(self.dma_tot.keys()):
                sems[k] = es2.enter_context(nc.semaphore(f"semd{i}"))
            block = es2.enter_context(nc.Block())

            def run(ename, engobj):
                for (kind, fn, waits, key) in self.q[ename]:
                    for (k, v) in waits:
                        engobj.wait_ge(sems[k], v)
                    if kind == "op":
                        fn(engobj).then_inc(sems[ename], 1)
                    elif kind == "dma":
                        fn(engobj).then_inc(sems[key], 16)

            @block.tensor
            def _(e):
                run("pe", e)

            @block.scalar
            def _(e):
                run("act", e)

            @block.vector
            def _(e):
                run("dve", e)

            @block.gpsimd
            def _(e):
                run("pool", e)

            @block.sync
            def _(e):
                run("sp", e)


class Arena:
    def __init__(self, S, nbytes):
        self.t = S.sb([128, nbytes // 4], F32, "arena")
        self.cap = nbytes
        self.off = 0

    def mark(self):
        return self.off

    def reset(self, m):
        self.off = m

    def alloc(self, free_shape, dt=F32, parts=128, name=""):
        n = int(np.prod(free_shape))
        sz = 2 if dt == BF16 else 4
        nb = (n * sz + 63) // 64 * 64
        assert self.off + nb <= self.cap, f"arena overflow allocating {name} {free_shape}: {self.off}+{nb}>{self.cap}"
        ap = self.t[0:parts, self.off // 4:(self.off + nb) // 4]
        if dt == BF16:
            ap = ap.bitcast(BF16)
        ap = ap[:, 0:n]
        if len(free_shape) == 2:
            ap = ap.rearrange("p (a b) -> p a b", a=free_shape[0])
        elif len(free_shape) == 3:
            ap = ap.rearrange("p (a b c) -> p a b c", a=free_shape[0], b=free_shape[1])
        elif len(free_shape) == 4:
            ap = ap.rearrange("p (a b c d) -> p a b c d", a=free_shape[0], b=free_shape[1], c=free_shape[2])
        self.off += nb
        return T(ap, name)


CM_OFF = {}


def _const_mats():
    idx = np.arange(128)
    s = idx[:, None]
    t = idx[None, :]
    same = (s // 64) == (t // 64)
    mats = {
        "ident": (s == t),
        "tri": same & (s <= t),
        "blk": same,
        "negtri": -1.0 * (same & (s <= t)),
        "striu": same & (s < t),
        "nstriu": -1.0 * (same & (s < t)),
        "nstril": -1.0 * (same & (s > t)),
        "ones": np.ones((128, 128)),
        "trifull": (s <= t),
    }
    cols = []
    off = 0
    for k, v in mats.items():
        CM_OFF[k] = off
        cols.append(np.asarray(v, np.float32))
        off += 128
    sel = np.zeros((128, 2), np.float32)
    sel[:64, 0] = 1
    sel[64:, 1] = 1
    CM_OFF["sel"] = off
    cols.append(sel)
    off += 2
    CM_OFF["nsel"] = off
    cols.append(-sel)
    off += 2
    return np.concatenate(cols, axis=1), off


CM_NP, CM_W = _const_mats()

RV = {}
_o = 0
for _n, _w in (("hng", 256), ("mu", 1056), ("w0", 256), ("a0", 256), ("kk", 256),
               ("ka", 256), ("rk", 256), ("lnw", 256), ("lnb", 256), ("v0", 256), ("lgam", 256)):
    RV[_n] = (_o, _w)
    _o += _w
RV_W = _o


def _rope_table(TP):
    half = 32
    inv_freq = (10000.0 ** (-np.arange(half, dtype=np.float32) / half)).astype(np.float32)
    pos = np.arange(TP, dtype=np.float32)
    ang = (pos[:, None] * inv_freq[None, :]).astype(np.float32)
    c = np.cos(ang).astype(np.float32)
    s = np.sin(ang).astype(np.float32)
    sc = np.float32(64 ** -0.5)
    tab = np.stack([np.stack([c, s], 1), np.stack([c * sc, s * sc], 1)], 1)
    return np.ascontiguousarray(tab.reshape(TP, 128)).astype(np.float32)


def _lgam_row():
    h = np.arange(4, dtype=np.float32)
    lg = np.log1p(-np.exp2(-5.0 - h)).astype(np.float32)
    return np.repeat(lg, 64).astype(np.float32)


def build(NT, NL=2, dbg=False):
    TP = NT * 128
    NB = 384 if NT % 3 == 0 else (256 if NT % 2 == 0 else 128)
    NBLK = TP // NB
    nc = bass.Bass("TRN2", target_bir_lowering=False)

    def din(name, shape, dt=F32):
        return nc.dram_tensor(name, list(shape), dt, kind="ExternalInput").ap()

    hT0 = din("hT0", [D, TP])
    w_in = din("w_in", [2, D, NIN])
    w_out = din("w_out", [2, D, D])
    w_up = din("w_up", [2, D, DFF])
    w_dn = din("w_dn", [2, DFF, D])
    gvec = din("gvec", [128, 40])
    rowv = din("rowv", [2, RV_W])
    lgts = din("lgts", [2, 256])
    s5vec = din("s5vec", [2, 128, 24])
    s5B = din("s5B", [2, 128, 2048])
    s5C = din("s5C", [2, 128, 2048])
    s5fm = din("s5fm", [2, 128, 4])
    glu_w = din("glu_w", [2, 256, 256])
    rw_wup = din("rw_wup", [2, 64, 256])
    rw_aup = din("rw_aup", [2, 64, 256])
    rw_gup = din("rw_gup", [2, 160, 256])
    rw_vdn = din("rw_vdn", [256, 32])
    rw_vup = din("rw_vup", [32, 256])
    cmat = din("cmat", [128, CM_W])
    rope = din("rope", [TP, 128])
    outT = nc.dram_tensor("outT", [D, TP], F32, kind="ExternalOutput").ap()
    hA = nc.dram_tensor("hA", [D, TP], F32, kind="Internal").ap()
    hB = nc.dram_tensor("hB", [D, TP], F32, kind="Internal").ap()
    vfirst = nc.dram_tensor("vfirst", [TP, 256], F32, kind="Internal").ap()
    zdram = nc.dram_tensor("zdram", [TP + 1, 3104], F32, kind="Internal").ap()
    udram = nc.dram_tensor("udram", [256, TP], F32, kind="Internal").ap()
    if dbg:
        dbg_o = nc.dram_tensor("dbg_o", [NL, D, TP], F32, kind="ExternalOutput").ap()
        dbg_h = nc.dram_tensor("dbg_h", [NL, D, TP], F32, kind="ExternalOutput").ap()

    with ExitStack() as es:
        S = Sched(nc, es)
        AR = Arena(S, 212000)
        PS = [T(S.ps([128, 512], F32, f"psb{i}"), f"ps{i}") for i in range(8)]

        def fsz(ap):
            n = 1
            for d_ in ap.shape[1:]:
                n *= int(d_)
            return n

        def ecost(eng, ap):
            n = fsz(ap)
            if eng == "pool":
                return 0.3 + 0.0028 * n
            return 0.2 + 0.00105 * n

        def mm(out, lhsT, rhs, R, W, start=True, stop=True):
            c_ = 0.03 + 0.00045 * max(64, fsz(rhs))
            if rhs.dtype == F32:
                c_ *= 4
            S.op("pe", lambda e: e.matmul(out, lhsT=lhsT, rhs=rhs, start=start, stop=stop), R, W, cost=c_)

        def tr(out, in_, ident, R, W):
            S.op("pe", lambda e: e.transpose(out=out, in_=in_, identity=ident), R, W, cost=0.07)

        def act(out, in_, func, R, W, bias=None, scale=None, eng="act"):
            kw = {}
            if bias is not None:
                kw["bias"] = bias
            if scale is not None:
                kw["scale"] = scale
            S.op(eng, lambda e: e.activation(out=out, in_=in_, func=func, **kw), R, W, cost=ecost(eng, out) + 0.15)

        def tt(out, in0, in1, op, R, W, eng="dve"):
            S.op(eng, lambda e: e.tensor_tensor(out=out, in0=in0, in1=in1, op=op), R, W, cost=ecost(eng, out))

        def ts(out, in0, s1, s2, op0, op1, R, W, eng="dve"):
            if s2 is None:
                S.op(eng, lambda e: e.tensor_single_scalar(out=out, in_=in0, scalar=s1, op=op0), R, W, cost=ecost(eng, out))
            else:
                S.op(eng, lambda e: e.tensor_scalar(out=out, in0=in0, scalar1=s1, scalar2=s2, op0=op0, op1=op1), R, W,
                     cost=ecost(eng, out))

        def stt(out, in0, scalar, in1, op0, op1, R, W, eng="dve"):
            S.op(eng, lambda e: e.scalar_tensor_tensor(out=out, in0=in0, scalar=scalar, in1=in1, op0=op0, op1=op1), R, W,
                 cost=ecost(eng, out))

        def cp(out, in_, R, W, eng="dve"):
            if eng == "act":
                S.op(eng, lambda e: e.activation(out=out, in_=in_, func=AF.Copy), R, W, cost=ecost(eng, out))
            else:
                S.op(eng, lambda e: e.tensor_copy(out=out, in_=in_), R, W, cost=ecost(eng, out))

        def recip(out, in_, R, W):
            S.op("dve", lambda e: e.reciprocal(out=out, in_=in_), R, W, cost=ecost("dve", out) + 0.1)

        def red(out, in_, R, W):
            S.op("dve", lambda e: e.tensor_reduce(out=out, in_=in_, axis=AX.X, op=ALU.add), R, W, cost=ecost("dve", in_))

        def memset(t, val, eng="pool"):
            S.op(eng, lambda e: e.memset(t.ap, val), [], [t])

        def dma(out, in_, R, W, eng="sp", **kw):
            S.dma(eng, lambda e: e.dma_start(out=out, in_=in_, **kw), R, W)

        CM = AR.alloc([CM_W], F32, name="cm")
        CMB = AR.alloc([256], BF16, name="cmb")
        GV = AR.alloc([40], F32, name="gvec")
        dma(CM.ap, cmat[:, :], [], [CM])
        dma(GV.ap, gvec[:, :], [], [GV])
        cp(CMB[:, 0:128], CM[:, CM_OFF["ident"]:CM_OFF["ident"] + 128], [CM], [CMB])
        cp(CMB[:, 128:256], CM[:, CM_OFF["trifull"]:CM_OFF["trifull"] + 128], [CM], [CMB])
        ZRO = AR.alloc([128], BF16, name="zeros")
        memset(ZRO, 0.0)
        for i_ in range(8):
            for q_ in range(4):
                S.op("pe", lambda e, i_=i_, q_=q_: e.matmul(PS[i_][:, q_ * 128:(q_ + 1) * 128], lhsT=ZRO[:, 0:128],
                                                            rhs=ZRO[:, 0:128], start=True, stop=True), [ZRO], [PS[i_]])

        def cm(name, w=128, bf=False, rows=128):
            if bf:
                o = {"ident": 0, "trifull": 128}[name]
                return CMB[0:rows, o:o + w]
            o = CM_OFF[name]
            return CM[0:rows, o:o + w]

        Hs = {}
        for m in ("ret", "hgrn", "rwkv"):
            Hs[m] = (AR.alloc([4, 64], F32, parts=64, name=f"H_{m}"), AR.alloc([4, 64], BF16, parts=64, name=f"Hb_{m}"))
        S5car = AR.alloc([16], F32, name="s5carry")
        ST5 = AR.alloc([16], F32, name="st5")
        pmark = AR.mark()

        hdram = {"src": None}
        DR = {"hA": T(hA, "hA"), "hB": T(hB, "hB"), "vf": T(vfirst, "vf"), "out": T(outT, "out")}
        ZD = [T(zdram, f"zd{i}", semkey=f"zd{i % 2}") for i in range(NT)]
        ZD0 = T(zdram, "zd0")
        UD = [T(udram, f"ud{i}", semkey=f"ud{i % 2}") for i in range(NT)]
        VF = [T(vfirst, f"vf{i}", semkey=f"vf{i % 2}") for i in range(NT)]
        if dbg:
            DR["dbg_o"] = T(dbg_o, "dbg_o")
            DR["dbg_h"] = T(dbg_h, "dbg_h")
        H0 = T(hT0, "hT0")

        def zproj_phase(l, src, src_ap):
            AR.reset(pmark)
            WIN = AR.alloc([8, NIN], BF16, name="win")
            wchunks = [(n0, min(512, NIN - n0)) for n0 in range(0, NIN, 512)]
            WINg = [T(WIN.ap, f"wing{k}") for k in range(len(wchunks))]
            for k, (n0, n) in enumerate(wchunks):
                for c in range(8):
                    dma(WIN[:, c, n0:n0 + n], w_in[l, c * 128:(c + 1) * 128, n0:n0 + n], [], [WINg[k]], eng="pool")
            HT2 = [AR.alloc([8, 128], F32, name=f"zp_ht{i}") for i in range(2)]
            HB2 = [AR.alloc([8, 128], BF16, name=f"zp_hb{i}") for i in range(2)]
            SQ2 = [AR.alloc([8, 128], F32, name=f"zp_sq{i}") for i in range(2)]
            ZT2 = [AR.alloc([3104], F32, name=f"zp_zt{i}") for i in range(2)]
            UT2 = [AR.alloc([2, 128], F32, name=f"zp_ut{i}") for i in range(2)]
            RS2 = [AR.alloc([1], F32, name=f"zp_rs{i}") for i in range(2)]
            RR2 = [AR.alloc([128], F32, name=f"zp_rr{i}") for i in range(2)]
            if l == 0:
                S.op("pool", lambda e: e.memset(ZT2[1][0:1, :], 0.0), [], [ZT2[1]])
                dma(zdram[0:1, :], ZT2[1][0:1, :], [ZT2[1]], [ZD0])
            g8 = GV[:, l * 8:(l + 1) * 8]
            chunks = [(n0, min(512, 3104 - n0)) for n0 in range(0, 3104, 512)]
            for it in range(NT):
                s_ = it % 2
                tsl = slice(it * 128, (it + 1) * 128)
                HT_, HB_, SQ_, ZT_, UT_, RS_, RR_ = HT2[s_], HB2[s_], SQ2[s_], ZT2[s_], UT2[s_], RS2[s_], RR2[s_]
                dma(HT_.ap, src_ap.rearrange("(c p) t -> p c t", p=128)[:, :, tsl], [src], [HT_])
                act(SQ_.ap, HT_.ap, AF.Square, [HT_], [SQ_])
                tt(HB_.ap, HT_.ap, g8.unsqueeze(2).broadcast_to([128, 8, 128]), ALU.mult, [HT_, GV], [HB_],
                   eng="pool" if s_ else "dve")
                pst = PS[6 + s_]
                for cch in range(8):
                    mm(pst[:, 0:128], cm("ones"), SQ_[:, cch, :], [CM, SQ_], [pst], start=(cch == 0), stop=(cch == 7))
                for cch in range(8):
                    mm(pst[:, 128:129], SQ_[:, cch, :], cm("ones", 1), [CM, SQ_], [pst], start=(cch == 0), stop=(cch == 7))
                act(RR_.ap, pst[:, 0:128], AF.Sqrt, [pst], [RR_], bias=EPS, scale=1.0 / D)
                recip(RR_.ap, RR_.ap, [RR_], [RR_])
                act(RS_.ap, pst[:, 128:129], AF.Sqrt, [pst], [RS_], bias=EPS, scale=1.0 / D)
                recip(RS_.ap, RS_.ap, [RS_], [RS_])
                for k, (n0, n) in enumerate(chunks):
                    pb = PS[k % 5]
                    for cch in range(8):
                        mm(pb[:, 0:n], HB_[:, cch, :], WIN[:, cch, n0:n0 + n], [HB_, WINg[k]], [pb],
                           start=(cch == 0), stop=(cch == 7))
                    if k % 2 == 0:
                        act(ZT_[:, n0:n0 + n], pb[:, 0:n], AF.Copy, [pb, RS_], [ZT_], scale=RS_[:, 0:1])
                    else:
                        ts(ZT_[:, n0:n0 + n], pb[:, 0:n], RS_[:, 0:1], None, ALU.mult, None, [pb, RS_], [ZT_])
                pu_ = PS[5]
                for ct in range(2):
                    for cch in range(8):
                        mm(pu_[:, ct * 128:(ct + 1) * 128], WIN[:, cch, 3104 + ct * 128:3104 + (ct + 1) * 128], HB_[:, cch, :],
                           [WINg[6], HB_], [pu_], start=(cch == 0), stop=(cch == 7))
                tt(UT_.ap, pu_[:, 0:256].rearrange("p (a b) -> p a b", a=2),
                   RR_.ap.unsqueeze(1).broadcast_to([128, 2, 128]), ALU.mult, [pu_, RR_], [UT_])
                dma(zdram[1 + it * 128:1 + (it + 1) * 128, :], ZT_.ap, [ZT_], [ZD[it]])
                dma(udram.rearrange("(c p) t -> p c t", p=128)[:, :, tsl], UT_.ap, [UT_], [UD[it]])
            S.barrier()

        def mixer_phase(l, src, src_ap, dst, dst_ap):
            AR.reset(pmark)
            WOUT = AR.alloc([8, D], BF16, name="wout")
            for c in range(8):
                dma(WOUT[:, c, :], w_out[l, c * 128:(c + 1) * 128, :], [], [WOUT], eng="pool")
            GLW = AR.alloc([2, 256], BF16, name="glw")
            dma(GLW.ap, glu_w[l].rearrange("(c p) n -> p c n", p=128), [], [GLW], eng="pool")
            WUPr = AR.alloc([256], BF16, parts=64, name="rwwup")
            AUPr = AR.alloc([256], BF16, parts=64, name="rwaup")
            GUPr = AR.alloc([2, 256], BF16, name="rwgup")
            dma(WUPr.ap, rw_wup[l], [], [WUPr], eng="pool")
            dma(AUPr.ap, rw_aup[l], [], [AUPr], eng="pool")
            dma(GUPr[:, 0, :], rw_gup[l, 0:128, :], [], [GUPr], eng="pool")
            dma(GUPr[0:32, 1, :], rw_gup[l, 128:160, :], [], [GUPr], eng="pool")
            if l > 0:
                VDN = AR.alloc([2, 32], BF16, name="vdn")
                VUP = AR.alloc([256], BF16, parts=32, name="vup")
                dma(VDN.ap, rw_vdn.rearrange("(c p) n -> p c n", p=128), [], [VDN], eng="pool")
                dma(VUP.ap, rw_vup[:, :], [], [VUP], eng="pool")
            ROW = AR.alloc([RV_W], F32, name="rowv")
            dma(ROW.ap, rowv[l:l + 1, :].broadcast_to([128, RV_W]), [], [ROW])

            def row(name, lo=0, w=None):
                o, ww = RV[name]
                w = ww if w is None else w
                return ROW[:, o + lo:o + lo + w]

            LB = AR.alloc([256], F32, name="lb")
            OML = AR.alloc([256], F32, name="oml")
            dma(LB.ap, lgts[1:2, :].broadcast_to([128, 256]), [], [LB])
            dma(OML.ap, lgts[0:1, :].broadcast_to([128, 256]), [], [OML])
            tt(LB.ap, LB.ap, OML.ap, ALU.subtract, [LB, OML], [LB])
            act(LB.ap, LB.ap, AF.Sigmoid, [LB], [LB])
            ts(LB.ap, LB.ap, float(l), None, ALU.mult, None, [LB], [LB])
            ts(OML.ap, LB.ap, -1.0, 1.0, ALU.mult, ALU.add, [LB], [OML])

            S5V = AR.alloc([24], F32, name="s5v")
            S5F = AR.alloc([4], F32, name="s5fm")
            dma(S5V.ap, s5vec[l], [], [S5V])
            dma(S5F.ap, s5fm[l], [], [S5F])
            BBLK = AR.alloc([2048], BF16, name="bblk")
            dma(BBLK.ap, s5B[l], [], [BBLK], eng="pool")
            KC = AR.alloc([2048], BF16, name="kc")
            QT = AR.alloc([16, 128], BF16, name="qt")
            QL = AR.alloc([16], F32, name="ql")
            CB = AR.alloc([16, 128], BF16, name="cb")
            SV = AR.alloc([24, 8], F32, name="s5small")
            m5 = AR.mark()
            KCT = AR.alloc([16, 128], F32, name="kct")
            QTF = AR.alloc([16, 128], F32, name="qtf")
            CST = AR.alloc([2048], F32, name="cstage")
            TMPA = AR.alloc([8, 128], F32, name="tmpa")
            TMPB = AR.alloc([8, 128], F32, name="tmpb")
            dma(CST.ap, s5C[l], [], [CST])
            a_re, a_im, ldt = S5V[:, 0:8], S5V[:, 8:16], S5V[:, 16:24]
            sv = lambda i: SV[:, i, :]
            RW_ = [S5V, SV]
            act(sv(0), ldt, AF.Exp, RW_, [SV])
            tt(sv(1), a_re, sv(0), ALU.mult, RW_, [SV])
            tt(sv(2), a_im, sv(0), ALU.mult, RW_, [SV])
            act(sv(3), sv(1), AF.Exp, RW_, [SV])
            act(sv(4), sv(1), AF.Exp, RW_, [SV], scale=-1.0)
            act(sv(5), sv(2), AF.Sin, RW_, [SV], scale=1.0 / 16)
            act(sv(6), sv(2), AF.Sin, RW_, [SV], scale=1.0 / 8)
            tt(sv(7), sv(5), sv(5), ALU.mult, RW_, [SV])
            ts(sv(7), sv(7), -2.0, 1.0, ALU.mult, ALU.add, RW_, [SV])
            for _ in range(3):
                tt(sv(8), sv(7), sv(7), ALU.mult, RW_, [SV])
                tt(sv(9), sv(6), sv(6), ALU.mult, RW_, [SV])
                stt(sv(6), sv(7), 2.0, sv(6), ALU.mult, ALU.mult, RW_, [SV])
                tt(sv(7), sv(8), sv(9), ALU.subtract, RW_, [SV])
            tt(sv(10), sv(7), sv(3), ALU.mult, RW_, [SV])
            tt(sv(11), sv(6), sv(3), ALU.mult, RW_, [SV])
            tt(sv(12), sv(7), sv(4), ALU.mult, RW_, [SV])
            stt(sv(13), sv(6), -1.0, sv(4), ALU.mult, ALU.mult, RW_, [SV])
            ts(sv(16), sv(10), -1.0, None, ALU.add, None, RW_, [SV])
            tt(sv(17), a_re, a_re, ALU.mult, RW_, [SV])
            tt(sv(18), a_im, a_im, ALU.mult, RW_, [SV])
            tt(sv(17), sv(17), sv(18), ALU.add, RW_, [SV])
            recip(sv(17), sv(17), RW_, [SV])
            tt(sv(18), sv(16), a_re, ALU.mult, RW_, [SV])
            tt(sv(19), sv(11), a_im, ALU.mult, RW_, [SV])
            tt(sv(18), sv(18), sv(19), ALU.add, RW_, [SV])
            tt(sv(14), sv(18), sv(17), ALU.mult, RW_, [SV])
            tt(sv(18), sv(11), a_re, ALU.mult, RW_, [SV])
            tt(sv(19), sv(16), a_im, ALU.mult, RW_, [SV])
            tt(sv(18), sv(18), sv(19), ALU.subtract, RW_, [SV])
            tt(sv(15), sv(18), sv(17), ALU.mult, RW_, [SV])

            def powtab(TAB, sre, sim):
                memset_ap = TAB[:, 0:8, 0:1]
                S.op("dve", lambda e: e.memset(memset_ap, 1.0), [], [TAB])
                memset_ap2 = TAB[:, 8:16, 0:1]
                S.op("dve", lambda e: e.memset(memset_ap2, 0.0), [], [TAB])
                cp(sv(20), sv(sre), RW_, [SV])
                cp(sv(21), sv(sim), RW_, [SV])
                n = 1
                while n < 128:
                    pre = TAB[:, 0:8, 0:n]
                    pim = TAB[:, 8:16, 0:n]
                    bre = SV[:, 20, :].unsqueeze(2).broadcast_to([128, 8, n])
                    bim = SV[:, 21, :].unsqueeze(2).broadcast_to([128, 8, n])
                    t1 = TMPA[:, :, 0:n]
                    t2 = TMPB[:, :, 0:n]
                    tt(t1, pre, bre, ALU.mult, [TAB, SV], [TMPA])
                    tt(t2, pim, bim, ALU.mult, [TAB, SV], [TMPB])
                    tt(TAB[:, 0:8, n:2 * n], t1, t2, ALU.subtract, [TMPA, TMPB], [TAB])
                    tt(t1, pre, bim, ALU.mult, [TAB, SV], [TMPA])
                    tt(t2, pim, bre, ALU.mult, [TAB, SV], [TMPB])
                    tt(TAB[:, 8:16, n:2 * n], t1, t2, ALU.add, [TMPA, TMPB], [TAB])
                    tt(sv(22), sv(20), sv(20), ALU.mult, RW_, [SV])
                    tt(sv(23), sv(21), sv(21), ALU.mult, RW_, [SV])
                    stt(sv(21), sv(20), 2.0, sv(21), ALU.mult, ALU.mult, RW_, [SV])
                    tt(sv(20), sv(22), sv(23), ALU.subtract, RW_, [SV])
                    n *= 2

            powtab(QTF, 10, 11)
            cp(QT.ap, QTF.ap, [QTF], [QT], eng="act")
            cp(QL.ap, QTF[:, :, 127], [QTF], [QL], eng="dve")
            powtab(KCT, 12, 13)
            for j in range(16):
                pb = PS[j % 2]
                S.op("pe", lambda e, j=j, pb=pb: e.transpose(out=pb[:, 0:128], in_=KCT[:, j, :], identity=cm("ident")),
                     [KCT, CM], [pb])
                cp(KC[:, j * 128:(j + 1) * 128], pb[:, 0:128], [pb], [KC], eng="act" if j % 2 else "dve")
            zre = SV[:, 14, :].unsqueeze(2).broadcast_to([128, 8, 128])
            zim = SV[:, 15, :].unsqueeze(2).broadcast_to([128, 8, 128])
            cre = CST[:, 0:1024].rearrange("p (a b) -> p a b", a=8)
            cim = CST[:, 1024:2048].rearrange("p (a b) -> p a b", a=8)
            tt(TMPA.ap, cre, zre, ALU.mult, [CST, SV], [TMPA])
            tt(TMPB.ap, cim, zim, ALU.mult, [CST, SV], [TMPB])
            tt(CB[:, 0:8, :], TMPA.ap, TMPB.ap, ALU.subtract, [TMPA, TMPB], [CB])
            tt(TMPA.ap, cre, zim, ALU.mult, [CST, SV], [TMPA])
            tt(TMPB.ap, cim, zre, ALU.mult, [CST, SV], [TMPB])
            stt(CB[:, 8:16, :], TMPA.ap, -1.0, TMPB.ap, ALU.mult, ALU.subtract, [TMPA, TMPB], [CB])
            S.barrier()
            AR.reset(m5)

            HT = AR.alloc([8, 128], F32, name="hT")
            ROPE = AR.alloc([128], F32, name="rope")
            OT = AR.alloc([8, 128], BF16, name="oT")
            OTOK = AR.alloc([768], BF16, name="otok")
            OTr = T(OTOK[:, 0:256], "otr")
            OTh = T(OTOK[:, 256:512], "oth")
            OTw = T(OTOK[:, 512:768], "otw")
            OT5 = T(OT[:, 6:8, :], "ot5")
            OT6 = T(OT[:, 0:6, :], "ot6")
            identb = cm("ident", bf=True)

            class Ctx:
                pass

            def mkctx(tag, delta, banks):
                c = Ctx()
                c.delta = delta
                c.R = AR.alloc([256], F32, name=tag + "R")
                c.K = AR.alloc([256], F32, name=tag + "K")
                c.V = AR.alloc([256], BF16, name=tag + "V")
                c.LW = AR.alloc([256], F32, name=tag + "LW")
                c.CUM = AR.alloc([256], F32, name=tag + "CUM")
                c.E = AR.alloc([2, 256], F32, name=tag + "E")
                c.TOK = AR.alloc([8 if delta else 4, 256], BF16, name=tag + "TOK")
                c.XT = AR.alloc([4 if delta else 2, 4, 128], BF16, parts=64, name=tag + "XT")
                c.SC = AR.alloc([6 if delta else 1, 4, 128], BF16, name=tag + "SC")
                c.WC = AR.alloc([4, 2], F32, parts=64, name=tag + "WC")
                c.O = AR.alloc([256], F32, name=tag + "O")
                c.TA = AR.alloc([256], F32, name=tag + "TA")
                c.TB = AR.alloc([256], F32, name=tag + "TB")
                c.ST = AR.alloc([16], F32, name=tag + "ST")
                c.HM = AR.alloc([4, 64], BF16, parts=64, name=tag + "HM")
                c.XTM = AR.alloc([2 if delta else 1, 4, 2, 128], BF16, parts=64, name=tag + "XTM")
                memset(c.XTM, 0.0)
                if delta:
                    c.KK = AR.alloc([256], F32, name=tag + "KK")
                    c.BB = AR.alloc([256], F32, name=tag + "BB")
                    c.SC2 = AR.alloc([2, 4, 128], BF16, name=tag + "SC2")
                    c.P1 = AR.alloc([256], BF16, name=tag + "P1")
                    c.U = AR.alloc([256], BF16, name=tag + "U")
                    memset(c.U, 0.0)
                    X, Y, Z = [PS[i] for i in banks]
                    c.bk = dict(cum=X, wc=(Y, 0), tr=(X, Y), sc=(X, Y, X, Y, Z), pa=X, pb=Y, pr=Z, pp=X, pu=Y,
                                ph=(Z, 0), po=(Y, 256))
                    c.bk["cum"] = Z
                else:
                    X, Y = [PS[i] for i in banks]
                    c.bk = dict(cum=X, wc=(Y, 0), tr=(X, Y), sc=(X,), ph=(X, 0), po=(Y, 0))
                return c

            CR = mkctx("r", False, (0, 1))
            CR.seqp = CR
            CH = CR
            CW0 = mkctx("w", True, (4, 5, 6))
            import copy as _copy
            CW1 = _copy.copy(CW0)
            for nm_, shp_, dt_, pr_ in (("SC", [6, 4, 128], BF16, 128), ("TOK", [8, 256], BF16, 128),
                                        ("XTM", [2, 4, 2, 128], BF16, 64), ("V", [256], BF16, 128),
                                        ("WC", [4, 2], F32, 64)):
                setattr(CW1, nm_, AR.alloc(shp_, dt_, parts=pr_, name="w1" + nm_))
            memset(CW1.XTM, 0.0)
            SQP = Ctx()
            SQP.HM = CW0.HM
            SQP.P1 = CW0.P1
            SQP.U = CW0.U
            SQP.O = CW0.O
            SQP.TA = AR.alloc([256], F32, name="sqTA")
            SQP.TB = AR.alloc([256], F32, name="sqTB")
            SQP.ST = AR.alloc([16], F32, name="sqST")
            SQP.bk = dict(pp=(PS[2], 0), pu=(PS[2], 256), ph=(PS[3], 0), po=(PS[2], 0))
            CW0.seqp = SQP
            CW1.seqp = SQP
            CW = [CW0, CW1]
            for cw_ in CW:
                cw_.RWV = AR.alloc([256], F32, name="rwV")
                cw_.RWG = AR.alloc([256], F32, name="rwG")
                cw_.STb = AR.alloc([4], F32, name="stb")
            P5 = PS[7]
            ZBr = AR.alloc([1024], F32, name="zbr")
            ZBh = AR.alloc([1024], F32, name="zbh")
            S5A = AR.alloc([2048], F32, name="s5a")
            S5T1 = AR.alloc([512], F32, name="s5t1")
            S5T2 = AR.alloc([512], F32, name="s5t2")
            S5E = AR.alloc([2048], BF16, name="s5e")
            S5XB = T(S5E.ap.rearrange("p (a b) -> p a b", a=16), "s5xb", buf=S5E.b)
            S5XE = AR.alloc([16], F32, name="s5xend")
            UT = AR.alloc([2, 128], F32, name="uT")
            UTB = AR.alloc([2, 128], BF16, name="uTb")
            S5Y = AR.alloc([2, 128], F32, name="s5y")
            S5G = AR.alloc([2, 128], F32, name="s5g")
            S5YB = AR.alloc([2, 128], BF16, name="s5yb")
            ZR = AR.alloc([1056], F32, name="zrwkv")
            PREV = AR.alloc([1056], F32, name="zprev")
            TC = AR.alloc([256], F32, name="tC")
            TD = AR.alloc([256], F32, name="tD")
            TRB = AR.alloc([512], BF16, name="trb")
            TRT = AR.alloc([4, 128], BF16, name="trt")

            def zrows(it, shift=0):
                return slice(1 + it * 128 - shift, 1 + (it + 1) * 128 - shift)

            def load_ret(it):
                dma(ZBr.ap, zdram[zrows(it), 0:1024], [ZD[it]], [ZBr])
                dma(ROPE.ap, rope[it * 128:(it + 1) * 128, :], [], [ROPE])

            def load_hgrn(it):
                dma(ZBh.ap, zdram[zrows(it), 1024:2048], [ZD[it]], [ZBh])

            def load_zr(it):
                dma(ZR.ap, zdram[zrows(it), 2048:3104], [ZD[it]], [ZR])

            def load_prev(it):
                rd = [ZD[it]] + ([ZD[it - 1]] if it > 0 else [ZD0])
                dma(PREV.ap, zdram[zrows(it, 1), 2048:3104], rd, [PREV])

            def load_ut(it):
                dma(UT.ap, udram.rearrange("(c p) t -> p c t", p=128)[:, :, it * 128:(it + 1) * 128], [UD[it]], [UT])

            def gla_tile(c, mix, r_scale, stage="all"):
                delta = c.delta
                if stage in ("all", "prep"):
                    yield from gla_prep(c, mix, r_scale)
                if stage in ("all", "seq"):
                    yield from gla_seq(c, mix)

            def gla_prep(c, mix, r_scale):
                delta = c.delta
                H, Hb = Hs[mix]
                bk = c.bk
                gR, gK, gV, gLW, gCUM, gE, gTOK, gXT, gSC, gWC, gO, XTM = (c.R, c.K, c.V, c.LW, c.CUM, c.E, c.TOK, c.XT,
                                                                            c.SC, c.WC, c.O, c.XTM)
                pcum = bk["cum"]
                mm(pcum[:, 0:256], cm("tri"), gLW.ap, [CM, gLW], [pcum])
                mm(pcum[:, 256:512], cm("blk"), gLW.ap, [CM, gLW], [pcum])
                pwc, wco = bk["wc"]
                for h in range(4):
                    mm(pwc[0:64, wco + h * 2:wco + h * 2 + 2], gLW[:, h * 64:(h + 1) * 64], cm("sel", 2), [gLW, CM], [pwc])
                yield
                act(gWC.ap.rearrange("p a b -> p (a b)"), pwc[0:64, wco:wco + 8], AF.Exp, [pwc], [gWC])
                cp(gCUM.ap, pcum[:, 0:256], [pcum], [gCUM], eng="act")
                act(gE[:, 0, :], gCUM.ap, AF.Exp, [gCUM], [gE])
                act(gE[:, 1, :], gCUM.ap, AF.Exp, [gCUM], [gE], scale=-1.0)
                yield
                if r_scale != 1.0:
                    stt(gTOK[:, 0, :], gR.ap, r_scale, gE[:, 0, :], ALU.mult, ALU.mult, [gR, gE], [gTOK])
                else:
                    tt(gTOK[:, 0, :], gR.ap, gE[:, 0, :], ALU.mult, [gR, gE], [gTOK])
                tt(gTOK[:, 1, :], gK.ap, gE[:, 1, :], ALU.mult, [gK, gE], [gTOK])
                if delta:
                    gKK, gBB = c.KK, c.BB
                    tt(gTOK[:, 5, :], gBB.ap, gE[:, 1, :], ALU.mult, [gBB, gE], [gTOK])
                tt(gE[:, 0, :], pcum[:, 256:512], gCUM.ap, ALU.subtract, [pcum, gCUM], [gE])
                act(gE[:, 0, :], gE[:, 0, :], AF.Exp, [gE], [gE])
                yield
                stt(gTOK[:, 2, :], gK.ap, cm("sel", 1), gE[:, 0, :], ALU.mult, ALU.mult, [gK, gE, CM], [gTOK])
                stt(gTOK[:, 3, :], gK.ap, CM[:, CM_OFF["sel"] + 1:CM_OFF["sel"] + 2], gE[:, 0, :], ALU.mult, ALU.mult,
                    [gK, gE, CM], [gTOK])
                tlist = [0, 1]
                if delta:
                    stt(gTOK[:, 6, :], gBB.ap, cm("nsel", 1), gE[:, 0, :], ALU.mult, ALU.mult, [gBB, gE, CM], [gTOK])
                    stt(gTOK[:, 7, :], gBB.ap, CM[:, CM_OFF["nsel"] + 1:CM_OFF["nsel"] + 2], gE[:, 0, :], ALU.mult,
                        ALU.mult, [gBB, gE, CM], [gTOK])
                    tt(gE[:, 1, :], gCUM.ap, gLW.ap, ALU.subtract, [gCUM, gLW], [gE])
                    act(gE[:, 1, :], gE[:, 1, :], AF.Exp, [gE], [gE])
                    tt(gTOK[:, 4, :], gKK.ap, gE[:, 1, :], ALU.mult, [gKK, gE], [gTOK])
                    tlist = [0, 1, 4, 5]
                yield
                for xi, tix in enumerate(tlist):
                    pb = bk["tr"][xi % 2]
                    pbv = pb.ap.bitcast(BF16)
                    for h in range(4):
                        tr(pbv[0:64, h * 128:(h + 1) * 128], gTOK[:, tix, h * 64:(h + 1) * 64], identb, [gTOK, CMB], [pb])
                    cp(gXT[:, xi, :, :], pbv[0:64, 0:512].rearrange("p (a b) -> p a b", a=4), [pb], [gXT], eng="act")
                    yield
                for cc in range(2):
                    cp(XTM[:, 0, :, cc, cc * 64:(cc + 1) * 64], gXT[:, 0, :, cc * 64:(cc + 1) * 64], [gXT], [XTM], eng="pool")
                    if delta:
                        cp(XTM[:, 1, :, cc, cc * 64:(cc + 1) * 64], gXT[:, 2, :, cc * 64:(cc + 1) * 64], [gXT], [XTM], eng="pool")

                def scores(dst_ix, lt, rt, mask, pb):
                    for h in range(4):
                        mm(pb[:, h * 128:(h + 1) * 128], gXT[:, lt, h, :], gXT[:, rt, h, :], [gXT], [pb])
                    tt(gSC[:, dst_ix, :, :], pb.ap.rearrange("p (a b) -> p a b", a=4),
                       cm(mask).unsqueeze(1).broadcast_to([128, 4, 128]), ALU.mult, [pb, CM], [gSC])

                scores(0, 1, 0, "tri", bk["sc"][0])
                yield
                if delta:
                    gSC2 = c.SC2
                    scores(3, 3, 2, "nstriu", bk["sc"][1])
                    scores(4, 2, 3, "nstril", bk["sc"][2])
                    yield
                    scores(1, 3, 0, "negtri", bk["sc"][3])
                    scores(2, 1, 2, "striu", bk["sc"][4])
                    tt(gSC[:, 5, :, :], gSC[:, 3, :, :], identb.unsqueeze(1).broadcast_to([128, 4, 128]), ALU.add,
                       [gSC, CMB], [gSC])
                    yield
                    Pc, PTc = (gSC, 3), (gSC, 4)
                    pa, pbb, pr = bk["pa"], bk["pb"], bk["pr"]
                    for j in range(1, 6):
                        Pn, PTn = ((gSC2, 0), (gSC2, 1)) if j % 2 == 1 else ((gSC, 3), (gSC, 4))
                        for h in range(4):
                            mm(pbb[:, h * 128:(h + 1) * 128], Pc[0][:, Pc[1], h, :], PTc[0][:, PTc[1], h, :],
                               [PTc[0], Pc[0]], [pbb])
                        if j < 5:
                            for h in range(4):
                                mm(pa[:, h * 128:(h + 1) * 128], PTc[0][:, PTc[1], h, :], Pc[0][:, Pc[1], h, :],
                                   [PTc[0], Pc[0]], [pa])
                        yield
                        cp(PTn[0][:, PTn[1], :, :], pbb.ap.rearrange("p (a b) -> p a b", a=4), [pbb], [PTn[0]], eng="dve")
                        if j < 5:
                            cp(Pn[0][:, Pn[1], :, :], pa.ap.rearrange("p (a b) -> p a b", a=4), [pa], [Pn[0]], eng="act")
                        for h in range(4):
                            mm(pr[:, h * 128:(h + 1) * 128], PTn[0][:, PTn[1], h, :], gSC[:, 5, h, :],
                               [PTn[0], gSC], [pr])
                        yield
                        tt(gSC[:, 5, :, :], gSC[:, 5, :, :], pr.ap.rearrange("p (a b) -> p a b", a=4), ALU.add,
                           [gSC, pr], [gSC])
                        Pc, PTc = Pn, PTn

            def gla_seq(c, mix):
                delta = c.delta
                H, Hb = Hs[mix]
                q = c.seqp
                gV, gTOK, gSC, gWC, XTM, gO = c.V, c.TOK, c.SC, c.WC, c.XTM, q.O
                if delta:
                    gP1, Ub = q.P1, q.U
                    pp, ppo = q.bk["pp"]
                    pu, puo = q.bk["pu"]
                    ph, pho = q.bk["ph"]
                    po, poo = q.bk["po"]
                else:
                    ph, pho = c.bk["ph"]
                    po, poo = c.bk["po"]
                HM = q.HM
                hmid32 = q.TA[0:64, :].rearrange("p (a b) -> p a b", a=4)
                hnew32 = q.TB[0:64, :].rearrange("p (a b) -> p a b", a=4)
                for cc in range(2):
                    if delta:
                        for h in range(4):
                            o_ = pp[:, ppo + h * 64:ppo + (h + 1) * 64]
                            mm(o_, gSC[:, 2, h, :], gV[:, h * 64:(h + 1) * 64], [gSC, gV], [pp], start=True, stop=False)
                            mm(o_, XTM[:, 1, h, 0, :], Hb[:, h, :], [XTM, Hb], [pp], start=False, stop=(cc == 0))
                            if cc == 1:
                                mm(o_, XTM[:, 1, h, 1, :], HM[:, h, :], [XTM, HM], [pp], start=False, stop=True)
                        yield
                        cp(gP1.ap, pp[:, ppo:ppo + 256], [pp], [gP1], eng="act")
                        for h in range(4):
                            mm(pu[:, puo + h * 64:puo + (h + 1) * 64], gSC[:, 5, h, :], gP1[:, h * 64:(h + 1) * 64],
                               [gSC, gP1], [pu])
                        yield
                        cp(Ub.ap, pu[:, puo:puo + 256], [pu], [Ub], eng="dve")
                    for h in range(4):
                        o_ = ph[0:64, pho + h * 64:pho + (h + 1) * 64]
                        mm(o_, gTOK[:, 2 + cc, h * 64:(h + 1) * 64], gV[:, h * 64:(h + 1) * 64], [gTOK, gV], [ph],
                           start=True, stop=not delta)
                        if delta:
                            mm(o_, gTOK[:, 6 + cc, h * 64:(h + 1) * 64], Ub[:, h * 64:(h + 1) * 64], [gTOK, Ub], [ph],
                               start=False, stop=True)
                    wc = gWC[:, :, cc:cc + 1].broadcast_to([64, 4, 64])
                    ph3 = ph[0:64, pho:pho + 256].rearrange("p (a b) -> p a b", a=4)
                    yield
                    if cc == 0:
                        tt(hmid32, H.ap, wc, ALU.mult, [H, gWC], [q.TA])
                        tt(hmid32, hmid32, ph3, ALU.add, [q.TA, ph], [q.TA])
                        cp(HM.ap, hmid32, [q.TA], [HM], eng="act")
                    else:
                        tt(hnew32, hmid32, wc, ALU.mult, [q.TA, gWC], [q.TB])
                        for h in range(4):
                            o_ = po[:, poo + h * 64:poo + (h + 1) * 64]
                            mm(o_, gSC[:, 0, h, :], gV[:, h * 64:(h + 1) * 64], [gSC, gV], [po], start=True, stop=False)
                            if delta:
                                mm(o_, gSC[:, 1, h, :], Ub[:, h * 64:(h + 1) * 64], [gSC, Ub], [po], start=False, stop=False)
                            mm(o_, XTM[:, 0, h, 0, :], Hb[:, h, :], [XTM, Hb], [po], start=False, stop=False)
                            mm(o_, XTM[:, 0, h, 1, :], HM[:, h, :], [XTM, HM], [po], start=False, stop=True)
                        yield
                        cp(gO.ap, po[:, poo:poo + 256], [po], [gO], eng="act")
                        tt(H.ap, hnew32, ph3, ALU.add, [q.TB, ph], [H])
                        cp(Hb.ap, H.ap, [H], [Hb], eng="act")
                    yield

            def rms_heads_to(c, dst_ap, dstT, extra_row, gate_src, gateT):
                gO, TA, TB, ST = c.O, c.TA, c.TB, c.ST
                o3 = gO.ap.rearrange("p (a b) -> p a b", a=4)
                act(TA.ap, gO.ap, AF.Square, [gO], [TA])
                red(ST[:, 0:4], TA.ap.rearrange("p (a b) -> p a b", a=4), [TA], [ST])
                act(ST[:, 0:4], ST[:, 0:4], AF.Sqrt, [ST], [ST], bias=EPS, scale=1.0 / 64)
                recip(ST[:, 0:4], ST[:, 0:4], [ST], [ST])
                yield
                tt(TA.ap.rearrange("p (a b) -> p a b", a=4), o3, ST[:, 0:4].unsqueeze(2).broadcast_to([128, 4, 64]),
                   ALU.mult, [gO, ST], [TA])
                if extra_row is not None:
                    tt(TA.ap, TA.ap, extra_row, ALU.mult, [TA, ROW], [TA])
                act(TB.ap, gate_src, AF.Silu, [gateT], [TB])
                tt(dst_ap, TA.ap, TB.ap, ALU.mult, [TA, TB], [dstT])
                yield

            def thread_ret(it):
                c = CR
                ZB = ZBr
                gR, gK, gV, gLW, TA, TB = c.R, c.K, c.V, c.LW, c.TA, c.TB
                qk = ZB[:, 0:512].rearrange("p (a h c d) -> p a h c d", a=2, h=4, c=2)
                rp = ROPE.ap.rearrange("p (a c d) -> p a c d", a=2, c=2)
                outqk = [gR.ap.rearrange("p (h c d) -> p h c d", h=4, c=2), gK.ap.rearrange("p (h c d) -> p h c d", h=4, c=2)]
                for a in range(2):
                    src_ = qk[:, a]
                    cos4 = rp[:, a, 0, :].unsqueeze(1).broadcast_to([128, 4, 32])
                    sin4 = rp[:, a, 1, :].unsqueeze(1).broadcast_to([128, 4, 32])
                    t1v = src_[:, :, 0, :]
                    t2v = src_[:, :, 1, :]
                    ta = TA[:, 0:128].rearrange("p (h d) -> p h d", h=4)
                    tb = TB[:, 0:128].rearrange("p (h d) -> p h d", h=4)
                    o_ = outqk[a]
                    dT = gR if a == 0 else gK
                    tt(ta, t1v, cos4, ALU.mult, [ZB, ROPE], [TA], eng="pool")
                    tt(tb, t2v, sin4, ALU.mult, [ZB, ROPE], [TB], eng="pool")
                    tt(o_[:, :, 0, :], ta, tb, ALU.subtract, [TA, TB], [dT], eng="pool")
                    tt(ta, t1v, sin4, ALU.mult, [ZB, ROPE], [TA], eng="pool")
                    tt(tb, t2v, cos4, ALU.mult, [ZB, ROPE], [TB], eng="pool")
                    tt(o_[:, :, 1, :], ta, tb, ALU.add, [TA, TB], [dT], eng="pool")
                    yield
                cp(gV.ap, ZB[:, 512:768], [ZB], [gV], eng="act")
                cp(gLW.ap, row("lgam"), [ROW], [gLW], eng="pool")
                yield from gla_tile(c, "ret", 1.0)
                yield from rms_heads_to(c, OTOK[:, 0:256], OTr, None, ZB[:, 768:1024], ZB)
                if it + 1 < NT:
                    load_ret(it + 1)
                yield

            def thread_hgrn(it):
                c = CH
                ZB = ZBh
                gR, gK, gV, gLW, TA, TB = c.R, c.K, c.V, c.LW, c.TA, c.TB
                act(TA.ap, ZB[:, 256:512], AF.Sigmoid, [ZB], [TA])
                tt(TA.ap, TA.ap, OML.ap, ALU.mult, [TA, OML], [TA])
                tt(TA.ap, TA.ap, LB.ap, ALU.add, [TA, LB], [TA])
                yield
                act(gLW.ap, TA.ap, AF.Ln, [TA], [gLW])
                act(gK.ap, TA.ap, AF.Identity, [TA], [gK], scale=-1.0, bias=1.0)
                act(gR.ap, ZB[:, 0:256], AF.Silu, [ZB], [gR])
                cp(gV.ap, ZB[:, 512:768], [ZB], [gV], eng="act")
                yield
                yield from gla_tile(c, "hgrn", 0.125)
                yield from rms_heads_to(c, OTOK[:, 256:512], OTh, row("hng"), ZB[:, 768:1024], ZB)
                if it + 1 < NT:
                    load_hgrn(it + 1)
                yield

            def thread_s5(it):
                cp(UTB.ap, UT.ap, [UT], [UTB], eng="act")
                yield
                for n in range(4):
                    kc = n % 2
                    mm(P5[:, :], UTB[:, kc, :], BBLK[:, kc * 1024 + (n // 2) * 512:kc * 1024 + (n // 2) * 512 + 512],
                       [UTB, BBLK], [P5])
                    cp(S5A[:, n * 512:(n + 1) * 512], P5[:, :], [P5], [S5A], eng="act")
                    yield
                bre, bim = S5A[:, 0:1024], S5A[:, 1024:2048]
                kre, kim = KC[:, 0:1024], KC[:, 1024:2048]
                for hf in range(2):
                    hs = slice(hf * 512, (hf + 1) * 512)
                    hs2 = slice(1024 + hf * 512, 1024 + (hf + 1) * 512)
                    tt(S5T1.ap, kre[:, hs], bre[:, hs], ALU.mult, [KC, S5A], [S5T1], eng="pool")
                    tt(S5T2.ap, kim[:, hs], bim[:, hs], ALU.mult, [KC, S5A], [S5T2], eng="pool")
                    tt(S5E[:, hs], S5T1.ap, S5T2.ap, ALU.subtract, [S5T1, S5T2], [S5E], eng="pool")
                    yield
                    tt(S5T1.ap, kre[:, hs], bim[:, hs], ALU.mult, [KC, S5A], [S5T1], eng="pool")
                    tt(S5T2.ap, kim[:, hs], bre[:, hs], ALU.mult, [KC, S5A], [S5T2], eng="pool")
                    tt(S5E[:, hs2], S5T1.ap, S5T2.ap, ALU.add, [S5T1, S5T2], [S5E], eng="pool")
                    yield
                zc = S5A.ap.rearrange("p (a b) -> p a b", a=16)
                for q in range(4):
                    for j in range(q * 4, q * 4 + 4):
                        mm(P5[:, (j % 4) * 128:(j % 4 + 1) * 128], S5E[:, j * 128:(j + 1) * 128], cm("trifull", bf=True),
                           [S5E, CMB], [P5])
                    tt(zc[:, q * 4:(q + 1) * 4, :], P5.ap.rearrange("p (a b) -> p a b", a=4),
                       S5car[:, q * 4:(q + 1) * 4].unsqueeze(2).broadcast_to([128, 4, 128]), ALU.add,
                       [P5, S5car], [S5A])
                    yield
                tt(ST5[:, 0:8], QL[:, 0:8], zc[:, 0:8, 127], ALU.mult, [QL, S5A], [ST5])
                tt(ST5[:, 8:16], QL[:, 8:16], zc[:, 8:16, 127], ALU.mult, [QL, S5A], [ST5])
                tt(S5XE[:, 0:8], ST5[:, 0:8], ST5[:, 8:16], ALU.subtract, [ST5], [S5XE])
                tt(ST5[:, 0:8], QL[:, 0:8], zc[:, 8:16, 127], ALU.mult, [QL, S5A], [ST5])
                tt(ST5[:, 8:16], QL[:, 8:16], zc[:, 0:8, 127], ALU.mult, [QL, S5A], [ST5])
                tt(S5XE[:, 8:16], ST5[:, 0:8], ST5[:, 8:16], ALU.add, [ST5], [S5XE])
                yield
                t1 = S5T1.ap.rearrange("p (a b) -> p a b", a=4)
                t2 = S5T2.ap.rearrange("p (a b) -> p a b", a=4)
                for hf in range(2):
                    a0, a1 = hf * 4, hf * 4 + 4
                    tt(t1, QT[:, a0:a1, :], zc[:, a0:a1, :], ALU.mult, [QT, S5A], [S5T1], eng="pool")
                    tt(t2, QT[:, 8 + a0:8 + a1, :], zc[:, 8 + a0:8 + a1, :], ALU.mult, [QT, S5A], [S5T2], eng="pool")
                    tt(S5XB[:, a0:a1, :], t1, t2, ALU.subtract, [S5T1, S5T2], [S5XB], eng="pool")
                    yield
                    tt(t1, QT[:, a0:a1, :], zc[:, 8 + a0:8 + a1, :], ALU.mult, [QT, S5A], [S5T1], eng="pool")
                    tt(t2, QT[:, 8 + a0:8 + a1, :], zc[:, a0:a1, :], ALU.mult, [QT, S5A], [S5T2], eng="pool")
                    tt(S5XB[:, 8 + a0:8 + a1, :], t1, t2, ALU.add, [S5T1, S5T2], [S5XB], eng="pool")
                    yield
                tt(ST5[:, 0:8], S5XE[:, 0:8], SV[:, 10, :], ALU.mult, [S5XE, SV], [ST5])
                tt(ST5[:, 8:16], S5XE[:, 8:16], SV[:, 11, :], ALU.mult, [S5XE, SV], [ST5])
                tt(S5car[:, 0:8], ST5[:, 0:8], ST5[:, 8:16], ALU.subtract, [ST5], [S5car])
                tt(ST5[:, 0:8], S5XE[:, 0:8], SV[:, 11, :], ALU.mult, [S5XE, SV], [ST5])
                tt(ST5[:, 8:16], S5XE[:, 8:16], SV[:, 10, :], ALU.mult, [S5XE, SV], [ST5])
                tt(S5car[:, 8:16], ST5[:, 0:8], ST5[:, 8:16], ALU.add, [ST5], [S5car])
                yield
                py = P5
                for ct in range(2):
                    k = 0
                    for reim in range(2):
                        for jj in range(ct * 4, ct * 4 + 4):
                            j = reim * 8 + jj
                            mm(py[:, ct * 128:(ct + 1) * 128], CB[:, j, :], S5XB[:, j, :], [CB, S5XB], [py],
                               start=(k == 0), stop=(k == 7))
                            k += 1
                yield
                for ct in range(2):
                    stt(S5Y[:, ct, :], UT[:, ct, :], S5F[:, ct:ct + 1], py[:, ct * 128:(ct + 1) * 128], ALU.mult, ALU.add,
                        [UT, S5F, py], [S5Y])
                if it + 1 < NT:
                    load_ut(it + 1)
                act(S5Y.ap, S5Y.ap, AF.Gelu, [S5Y], [S5Y])
                cp(S5YB.ap, S5Y.ap, [S5Y], [S5YB], eng="act")
                pg = P5
                for nt_ in range(2):
                    for kc in range(2):
                        mm(pg[:, 256 + nt_ * 128:256 + (nt_ + 1) * 128], GLW[:, kc, nt_ * 128:(nt_ + 1) * 128], S5YB[:, kc, :],
                           [GLW, S5YB], [pg], start=(kc == 0), stop=(kc == 1))
                yield
                for nt_ in range(2):
                    act(S5G[:, nt_, :], pg[:, 256 + nt_ * 128:256 + (nt_ + 1) * 128], AF.Sigmoid, [pg, S5F], [S5G],
                        bias=S5F[:, 2 + nt_:3 + nt_])
                tt(OT[:, 6:8, :], S5Y.ap, S5G.ap, ALU.mult, [S5Y, S5G], [OT5])
                yield

            def rwkv_prep(it):
                tsl = slice(it * 128, (it + 1) * 128)
                c = CW[it % 2]
                RWG, RWV = c.RWG, c.RWV
                gR, gK, gV, gLW, gKK, gBB, TA, TB, ST = c.R, c.K, c.V, c.LW, c.KK, c.BB, c.TA, c.TB, c.ST
                tt(PREV.ap, PREV.ap, ZR.ap, ALU.subtract, [PREV, ZR], [PREV])
                tt(PREV.ap, PREV.ap, row("mu"), ALU.mult, [PREV, ROW], [PREV])
                tt(ZR.ap, ZR.ap, PREV.ap, ALU.add, [ZR, PREV], [ZR])
                if it + 1 < NT:
                    load_prev(it + 1)
                yield
                r_ap, k_ap, v_ap = ZR[:, 0:256], ZR[:, 256:512], ZR[:, 512:768]
                act(TRB[:, 0:64], ZR[:, 768:832], AF.Tanh, [ZR], [TRB])
                cp(TRB[:, 64:128], ZR[:, 832:896], [ZR], [TRB], eng="dve")
                act(TRB[:, 128:288], ZR[:, 896:1056], AF.Sigmoid, [ZR], [TRB])
                pb = PS[4]
                pbv = pb.ap.bitcast(BF16)
                tr(pbv[0:64, 0:128], TRB[:, 0:64], identb, [TRB, CMB], [pb])
                tr(pbv[0:64, 128:256], TRB[:, 64:128], identb, [TRB, CMB], [pb])
                tr(pbv[:, 256:384], TRB[:, 128:256], identb, [TRB, CMB], [pb])
                tr(pbv[0:32, 384:512], TRB[:, 256:288], identb, [TRB, CMB], [pb])
                yield
                cp(TRT[0:64, 0:2, :], pbv[0:64, 0:256].rearrange("p (a b) -> p a b", a=2), [pb], [TRT], eng="act")
                cp(TRT[:, 2, :], pbv[:, 256:384], [pb], [TRT], eng="act")
                cp(TRT[0:32, 3, :], pbv[0:32, 384:512], [pb], [TRT], eng="act")
                pw_ = PS[5]
                mm(pw_[:, 0:256], TRT[0:64, 0, :], WUPr.ap, [TRT, WUPr], [pw_])
                mm(pw_[:, 256:512], TRT[0:64, 1, :], AUPr.ap, [TRT, AUPr], [pw_])
                pg_ = PS[6]
                mm(pg_[:, 0:256], TRT[:, 2, :], GUPr[:, 0, :], [TRT, GUPr], [pg_], start=True, stop=False)
                mm(pg_[:, 0:256], TRT[0:32, 3, :], GUPr[0:32, 1, :], [TRT, GUPr], [pg_], start=False, stop=True)
                yield
                cp(RWG.ap, pg_[:, 0:256], [pg_], [RWG], eng="act")
                tt(TA.ap, pw_[:, 0:256], row("w0"), ALU.add, [pw_, ROW], [TA])
                act(TA.ap, TA.ap, AF.Sigmoid, [TA], [TA])
                ts(gLW.ap, TA.ap, -float(np.exp(-0.5)), None, ALU.mult, None, [TA], [gLW])
                tt(TB.ap, pw_[:, 256:512], row("a0"), ALU.add, [pw_, ROW], [TB])
                act(TB.ap, TB.ap, AF.Sigmoid, [TB], [TB])
                yield
                if l == 0:
                    cp(RWV.ap, v_ap, [ZR], [RWV], eng="pool")
                    dma(vfirst[tsl, :], RWV.ap, [RWV], [VF[it]])
                else:
                    cp(TRB[:, 0:256], v_ap, [ZR], [TRB], eng="act")
                    pb = PS[4]
                    pbv = pb.ap.bitcast(BF16)
                    tr(pbv[:, 0:128], TRB[:, 0:128], identb, [TRB, CMB], [pb])
                    tr(pbv[:, 128:256], TRB[:, 128:256], identb, [TRB, CMB], [pb])
                    yield
                    cp(TRT[:, 0:2, :], pbv[:, 0:256].rearrange("p (a b) -> p a b", a=2), [pb], [TRT], eng="act")
                    pv = PS[5]
                    mm(pv[0:32, 0:128], VDN[:, 0, :], TRT[:, 0, :], [VDN, TRT], [pv], start=True, stop=False)
                    mm(pv[0:32, 0:128], VDN[:, 1, :], TRT[:, 1, :], [VDN, TRT], [pv], start=False, stop=True)
                    yield
                    cp(TRB[0:32, 256:384], pv[0:32, 0:128], [pv], [TRB], eng="act")
                    pv2 = PS[6]
                    mm(pv2[:, 0:256], TRB[0:32, 256:384], VUP.ap, [TRB, VUP], [pv2])
                    yield
                    tt(TC.ap, pv2[:, 0:256], row("v0"), ALU.add, [pv2, ROW], [TC])
                    act(TC.ap, TC.ap, AF.Sigmoid, [TC], [TC])
                    dma(TD.ap, vfirst[tsl, :], [VF[it]], [TD])
                    tt(TD.ap, TD.ap, v_ap, ALU.subtract, [TD, ZR], [TD])
                    tt(TD.ap, TD.ap, TC.ap, ALU.mult, [TD, TC], [TD])
                    tt(RWV.ap, TD.ap, v_ap, ALU.add, [TD, ZR], [RWV])
                cp(gV.ap, RWV.ap, [RWV], [gV], eng="act")
                yield
                tt(gKK.ap, k_ap, row("kk"), ALU.mult, [ZR, ROW], [gKK])
                act(TC.ap, gKK.ap, AF.Square, [gKK], [TC])
                red(ST[:, 0:4], TC.ap.rearrange("p (a b) -> p a b", a=4), [TC], [ST])
                act(ST[:, 0:4], ST[:, 0:4], AF.Sqrt, [ST], [ST])
                ts(ST[:, 0:4], ST[:, 0:4], 1e-12, None, ALU.max, None, [ST], [ST])
                recip(ST[:, 0:4], ST[:, 0:4], [ST], [ST])
                yield
                tt(gKK.ap.rearrange("p (a b) -> p a b", a=4), gKK.ap.rearrange("p (a b) -> p a b", a=4),
                   ST[:, 0:4].unsqueeze(2).broadcast_to([128, 4, 64]), ALU.mult, [gKK, ST], [gKK])
                tt(gBB.ap, gKK.ap, TB.ap, ALU.mult, [gKK, TB], [gBB])
                stt(TC.ap, TB.ap, -1.0, row("ka"), ALU.add, ALU.mult, [TB, ROW], [TC])
                stt(gK.ap, TC.ap, 1.0, k_ap, ALU.add, ALU.mult, [TC, ZR], [gK])
                cp(gR.ap, r_ap, [ZR], [gR], eng="pool")
                yield
                tt(TD.ap, gR.ap, gK.ap, ALU.mult, [gR, gK], [TD])
                tt(TD.ap, TD.ap, row("rk"), ALU.mult, [TD, ROW], [TD])
                red(c.STb[:, 0:4], TD.ap.rearrange("p (a b) -> p a b", a=4), [TD], [c.STb])
                if it + 1 < NT:
                    load_zr(it + 1)
                yield
                yield from gla_tile(c, "rwkv", 1.0, stage="prep")

            def rwkv_seq(it):
                c = CW[it % 2]
                q = c.seqp
                RWG, RWV = c.RWG, c.RWV
                TA, TC, ST = q.TA, q.TB, q.ST
                yield from gla_tile(c, "rwkv", 1.0, stage="seq")
                gO = q.O
                o3 = gO.ap.rearrange("p (a b) -> p a b", a=4)
                red(ST[:, 8:12], o3, [gO], [ST])
                ts(ST[:, 8:12], ST[:, 8:12], 1.0 / 64, None, ALU.mult, None, [ST], [ST])
                tt(TA.ap.rearrange("p (a b) -> p a b", a=4), o3, ST[:, 8:12].unsqueeze(2).broadcast_to([128, 4, 64]),
                   ALU.subtract, [gO, ST], [TA])
                act(TC.ap, TA.ap, AF.Square, [TA], [TC])
                red(ST[:, 12:16], TC.ap.rearrange("p (a b) -> p a b", a=4), [TC], [ST])
                yield
                act(ST[:, 12:16], ST[:, 12:16], AF.Sqrt, [ST], [ST], bias=64e-5, scale=1.0 / 64)
                recip(ST[:, 12:16], ST[:, 12:16], [ST], [ST])
                tt(TA.ap.rearrange("p (a b) -> p a b", a=4), TA.ap.rearrange("p (a b) -> p a b", a=4),
                   ST[:, 12:16].unsqueeze(2).broadcast_to([128, 4, 64]), ALU.mult, [TA, ST], [TA])
                yield
                tt(TA.ap, TA.ap, row("lnw"), ALU.mult, [TA, ROW], [TA])
                tt(TA.ap, TA.ap, row("lnb"), ALU.add, [TA, ROW], [TA])
                tt(TC.ap.rearrange("p (a b) -> p a b", a=4), RWV.ap.rearrange("p (a b) -> p a b", a=4),
                   c.STb[:, 0:4].unsqueeze(2).broadcast_to([128, 4, 64]), ALU.mult, [RWV, c.STb], [TC])
                tt(TA.ap, TA.ap, TC.ap, ALU.add, [TA, TC], [TA])
                tt(OTOK[:, 512:768], TA.ap, RWG.ap, ALU.mult, [TA, RWG], [OTw])
                yield

            for m in ("ret", "hgrn", "rwkv"):
                memset(Hs[m][0], 0.0)
                memset(Hs[m][1], 0.0)
            memset(S5car, 0.0)
            load_zr(0)
            load_prev(0)
            for _ in rwkv_prep(0):
                pass
            load_ret(0)
            load_hgrn(0)
            load_ut(0)

            for it in range(NT):
                tsl = slice(it * 128, (it + 1) * 128)
                dma(HT.ap, src_ap.rearrange("(c p) t -> p c t", p=128)[:, :, tsl], [src], [HT])
                import os
                _th = os.environ.get("KTH", "rwkv,s5,hgrn,ret").split(",")
                def thread_rh(it_):
                    yield from thread_ret(it_)
                    yield from thread_hgrn(it_)
                gens = [rwkv_seq(it)]
                if it + 1 < NT:
                    gens.append(rwkv_prep(it + 1))
                gens += [thread_s5(it), thread_ret(it), thread_hgrn(it)] if False else [thread_s5(it), thread_rh(it)]
                S.run_threads(gens)

                pb = PS[0]
                pbv = pb.ap.bitcast(BF16)
                for j in range(6):
                    tr(pbv[:, j * 128:(j + 1) * 128], OTOK[:, j * 128:(j + 1) * 128], identb, [OTr, OTh, OTw, CMB], [pb])
                cp(OT[:, 0:6, :], pbv[:, 0:768].rearrange("p (a b) -> p a b", a=6), [pb], [OT6], eng="act")
                if dbg:
                    cp(S5A[:, 0:1024], OT.ap.rearrange("p a b -> p (a b)"), [OT5, OT6], [S5A], eng="dve")
                    dma(dbg_o[l].rearrange("(c p) t -> p c t", p=128)[:, :, tsl],
                        S5A[:, 0:1024].rearrange("p (a b) -> p a b", a=8), [S5A], [DR["dbg_o"]])
                for half in range(2):
                    pb = PS[2 + half]
                    for q in range(4):
                        cidx = half * 4 + q
                        for cch in range(8):
                            mm(pb[:, q * 128:(q + 1) * 128], WOUT[:, cch, cidx * 128:(cidx + 1) * 128], OT[:, cch, :],
                               [WOUT, OT5, OT6], [pb], start=(cch == 0), stop=(cch == 7))
                    tt(HT[:, half * 4:(half + 1) * 4, :], HT[:, half * 4:(half + 1) * 4, :],
                       pb.ap.rearrange("p (a b) -> p a b", a=4), ALU.add, [HT, pb], [HT])
                dma(dst_ap.rearrange("(c p) t -> p c t", p=128)[:, :, tsl], HT.ap, [HT], [dst])
                if dbg:
                    dma(dbg_h[l].rearrange("(c p) t -> p c t", p=128)[:, :, tsl], HT.ap, [HT], [DR["dbg_h"]])
            S.barrier()

        def ffn_phase(l, src, src_ap, dst, dst_ap, final):
            AR.reset(pmark)
            WUP = AR.alloc([8, DFF], BF16, name="wup")
            WDN = AR.alloc([32, D], BF16, name="wdn")
            WUPg = [T(WUP.ap, f"wupg{g}") for g in range(4)]
            WDNg = [T(WDN.ap, f"wdng{g}") for g in range(4)]
            for g in range(4):
                for c in range(8):
                    dma(WUP[:, c, g * 1024:(g + 1) * 1024], w_up[l, c * 128:(c + 1) * 128, g * 1024:(g + 1) * 1024], [],
                        [WUPg[g]], eng="pool")
            for g in range(4):
                for c in range(g * 8, g * 8 + 8):
                    dma(WDN[:, c, :], w_dn[l, c * 128:(c + 1) * 128, :], [], [WDNg[g]], eng="pool")
            HT = AR.alloc([8, NB], F32, name="f_hT")
            HB = AR.alloc([8, NB], BF16, name="f_hb")
            SQ = AR.alloc([8, NB], F32, name="f_sq")
            RS = AR.alloc([NB], F32, name="f_rstd")
            HID = AR.alloc([32, NB], BF16, name="f_hid")
            TMP = [T(SQ[:, i, :], f"f_tmp{i}") for i in range(2)]
            gcol = 16 + l * 8
            for ib in range(NBLK):
                bsl = slice(ib * NB, (ib + 1) * NB)
                dma(HT.ap, src_ap.rearrange("(c p) t -> p c t", p=128)[:, :, bsl], [src], [HT])
                act(SQ.ap, HT.ap, AF.Square, [HT], [SQ, TMP[0], TMP[1]])
                tt(HB.ap, HT.ap, GV[:, gcol:gcol + 8].unsqueeze(2).broadcast_to([128, 8, NB]), ALU.mult, [HT, GV], [HB])
                pst = PS[0]
                for c in range(8):
                    mm(pst[:, 0:NB], cm("ones"), SQ[:, c, :], [CM, SQ], [pst], start=(c == 0), stop=(c == 7))
                act(RS.ap, pst[:, 0:NB], AF.Sqrt, [pst], [RS], bias=EPS, scale=1.0 / D)
                recip(RS.ap, RS.ap, [RS], [RS])
                for f in range(32):
                    pb = PS[1 + f % 3]
                    for c in range(8):
                        mm(pb[:, 0:NB], WUP[:, c, f * 128:(f + 1) * 128], HB[:, c, :], [WUPg[f // 8], HB], [pb],
                           start=(c == 0), stop=(c == 7))
                    tm = TMP[f % 2]
                    stt(tm.ap, pb[:, 0:NB], 0.0, RS.ap, ALU.max, ALU.mult, [pb, RS, SQ], [tm])
                    act(HID[:, f, :], tm.ap, AF.Square, [tm], [HID])
                for dt_ in range(8):
                    pb = PS[4 + dt_ % 3]
                    for f in range(32):
                        mm(pb[:, 0:NB], WDN[:, f, dt_ * 128:(dt_ + 1) * 128], HID[:, f, :], [WDNg[f // 8], HID], [pb],
                           start=(f == 0), stop=(f == 31))
                    tt(HT[:, dt_, :], HT[:, dt_, :], pb[:, 0:NB], ALU.add, [HT, pb], [HT])
                if not final:
                    dma(dst_ap.rearrange("(c p) t -> p c t", p=128)[:, :, bsl], HT.ap, [HT], [dst])
                else:
                    act(SQ.ap, HT.ap, AF.Square, [HT], [SQ, TMP[0], TMP[1]])
                    for c in range(8):
                        mm(pst[:, 0:NB], cm("ones"), SQ[:, c, :], [CM, SQ], [pst], start=(c == 0), stop=(c == 7))
                    act(RS.ap, pst[:, 0:NB], AF.Sqrt, [pst], [RS], bias=EPS, scale=1.0 / D)
                    recip(RS.ap, RS.ap, [RS], [RS])
                    tt(SQ.ap, HT.ap, GV[:, 32:40].unsqueeze(2).broadcast_to([128, 8, NB]), ALU.mult, [HT, GV], [SQ])
                    tt(SQ.ap, SQ.ap, RS.ap.unsqueeze(1).broadcast_to([128, 8, NB]), ALU.mult, [SQ, RS], [SQ])
                    dma(dst_ap.rearrange("(c p) t -> p c t", p=128)[:, :, bsl], SQ.ap, [SQ], [dst])
            S.barrier()

        cur, cur_ap = H0, hT0
        for l in range(NL):
            zproj_phase(l, cur, cur_ap)
            import os
            if os.environ.get("KSTOP") == "z":
                break
            mixer_phase(l, cur, cur_ap, DR["hA"], hA)
            if os.environ.get("KSTOP") == "m":
                break
            last = (l == NL - 1)
            if last:
                ffn_phase(l, DR["hA"], hA, DR["out"], outT, True)
            else:
                ffn_phase(l, DR["hA"], hA, DR["hB"], hB, False)
                cur, cur_ap = DR["hB"], hB
        S.barrier()
        S.emit()
    return nc


def host_inputs(inp, NT, b):
    TP = NT * 128
    f = np.float32
    x = np.asarray(inp["x"], f)
    h0 = np.zeros((TP, D), f)
    h0[:NMETA] = np.asarray(inp["meta_tokens"], f)
    nreal = min(SEQ, TP - NMETA)
    h0[NMETA:NMETA + nreal] = x[b, :nreal]
    m = {}
    m["hT0"] = np.ascontiguousarray(h0.T)
    m["w_in"] = np.asarray(inp["w_in"], f)
    m["w_out"] = np.asarray(inp["w_out"], f)
    m["w_up"] = np.asarray(inp["w_ffn_up"], f)
    m["w_dn"] = np.asarray(inp["w_ffn_down"], f)
    fm = lambda v: np.asarray(v, f).reshape(8, 128).T
    m["gvec"] = np.ascontiguousarray(np.concatenate(
        [fm(inp["norm_mix_g"][0]), fm(inp["norm_mix_g"][1]), fm(inp["norm_ffn_g"][0]), fm(inp["norm_ffn_g"][1]),
         fm(inp["norm_f_g"])], axis=1)).astype(f)
    rv = np.zeros((2, RV_W), f)
    for l in range(2):
        def put(name, v):
            o, w = RV[name]
            rv[l, o:o + w] = np.asarray(v, f).reshape(-1)
        put("hng", np.tile(np.asarray(inp["hgrn_norm_g"][l], f), 4))
        put("mu", inp["rwkv_mu"][l])
        put("w0", inp["rwkv_w0"][l])
        put("a0", inp["rwkv_a0"][l])
        put("kk", inp["rwkv_k_k"][l])
        put("ka", inp["rwkv_k_a"][l])
        put("rk", inp["rwkv_r_k"][l])
        put("lnw", inp["rwkv_ln_w"][l])
        put("lnb", inp["rwkv_ln_b"][l])
        if l > 0:
            put("v0", inp["rwkv_v0"][l - 1])
        put("lgam", _lgam_row())
    m["rowv"] = rv
    m["lgts"] = np.asarray(inp["hgrn_lb_logits"], f)
    def st(v):
        return np.asarray(v, f).reshape(8, 128).T
    s5vec = np.zeros((2, 128, 24), f)
    s5B = np.zeros((2, 128, 2, 2, 8, 64), f)
    s5C = np.zeros((2, 128, 2, 8, 8, 16), f)
    s5fm = np.zeros((2, 128, 4), f)
    for l in range(2):
        s5vec[l, :, 0:8] = st(inp["s5_a_re"][l])
        s5vec[l, :, 8:16] = st(inp["s5_a_im"][l])
        s5vec[l, :, 16:24] = st(np.repeat(np.asarray(inp["s5_log_dt"][l], f)[:, None], 64, axis=1))
        for ri, key in enumerate(("s5_b_re", "s5_b_im")):
            bb = np.asarray(inp[key][l], f)
            for g in range(16):
                kc, g8 = g // 8, g % 8
                s5B[l, g8 * 16:(g8 + 1) * 16, kc, ri, g8, :] = bb[g].T
        for ri, key in enumerate(("s5_c_re", "s5_c_im")):
            cc = np.asarray(inp[key][l], f)
            for g in range(16):
                jj, gl = g // 2, g % 2
                g8 = g % 8
                s5C[l, gl * 64:(gl + 1) * 64, ri, jj, g8, :] = cc[g].T
        s5fm[l, :, 0:2] = np.asarray(inp["s5_d"][l], f).reshape(2, 128).T
        s5fm[l, :, 2:4] = np.asarray(inp["s5_glu_b"][l], f).reshape(2, 128).T
    m["s5vec"] = s5vec
    m["s5B"] = s5B.reshape(2, 128, 2048)
    m["s5C"] = s5C.reshape(2, 128, 2048)
    m["s5fm"] = s5fm
    m["glu_w"] = np.asarray(inp["s5_glu_w"], f)
    m["rw_wup"] = np.asarray(inp["rwkv_w_up"], f)
    m["rw_aup"] = np.asarray(inp["rwkv_a_up"], f)
    m["rw_gup"] = np.asarray(inp["rwkv_g_up"], f)
    m["rw_vdn"] = np.asarray(inp["rwkv_v_down"][0], f)
    m["rw_vup"] = np.asarray(inp["rwkv_v_up"][0], f)
    m["cmat"] = CM_NP
    m["rope"] = _rope_table(TP)
    return m


_NC_CACHE = {}


def kernel(**inputs):
    NT = 33
    if NT not in _NC_CACHE:
        _NC_CACHE[NT] = build(NT)
    nc = _NC_CACHE[NT]
    shared = host_inputs(inputs, NT, 0)
    in_maps = []
    for b in range(8):
        mb = dict(shared)
        if b > 0:
            mb["hT0"] = host_inputs_h(inputs, NT, b)
        in_maps.append(mb)
    res = run_bass_kernel_spmd(nc, in_maps, core_ids=list(range(8)))
    out = np.empty((8, SEQ, D), np.float32)
    for b in range(8):
        out[b] = res.results[b]["outT"][:, NMETA:NMETA + SEQ].T
    return out


def host_inputs_h(inp, NT, b):
    TP = NT * 128
    h0 = np.zeros((TP, D), np.float32)
    h0[:NMETA] = np.asarray(inp["meta_tokens"], np.float32)
    h0[NMETA:NMETA + SEQ] = np.asarray(inp["x"], np.float32)[b]
    return np.ascontiguousarray(h0.T)
```

```python
import numpy as np
from contextlib import ExitStack
import concourse.bass as bass
import concourse.mybir as mybir
from concourse.bass_utils import run_bass_kernel_spmd

F32 = mybir.dt.float32
BF16 = mybir.dt.bfloat16
AF = mybir.ActivationFunctionType
ALU = mybir.AluOpType
AX = mybir.AxisListType

D = 1024
NMETA = 16
SEQ = 4096
NIN = 3360
DFF = 4096
EPS = 1e-6


class Buf:
    __slots__ = ("name", "last_w", "reads")

    def __init__(self, name=""):
        self.name = name
        self.last_w = None
        self.reads = []


class T:
    def __init__(self, ap, name="", buf=None, semkey=None):
        self.ap = ap
        self.b = buf if buf is not None else Buf(name)
        self.semkey = semkey

    def __getitem__(self, k):
        return self.ap[k]


class Sched:
    ENG = ("pe", "act", "dve", "pool", "sp")

    def __init__(self, nc, es):
        self.nc = nc
        self.es = es
        self.q = {e: [] for e in self.ENG}
        self.cnt = {e: 0 for e in self.ENG}
        self.seen = {e: {} for e in self.ENG}
        self.dma_tot = {}
        self.nt = 0
        self.capture = None
        self.eng_free = {e: 0.0 for e in self.ENG}
        self.buf_ready = {}

    def _model(self, eng, reads, writes, cost, lat):
        t = self.eng_free[eng]
        for x in list(reads) + list(writes):
            t = max(t, self.buf_ready.get(id(x.b), 0.0))
        return t

    def _commit(self, eng, reads, writes, cost, lat, t):
        self.eng_free[eng] = t + cost
        for x in writes:
            self.buf_ready[id(x.b)] = t + cost + lat
        for x in reads:
            self.buf_ready[id(x.b)] = max(self.buf_ready.get(id(x.b), 0.0), t + cost)

    def run_threads(self, gens):
        lists = []
        for g in gens:
            self.capture = []
            for _ in g:
                pass
            lists.append(self.capture)
            self.capture = None
        pos = [0] * len(lists)
        while True:
            best = None
            for i, L in enumerate(lists):
                if pos[i] >= len(L):
                    continue
                kind, eng, fn, reads, writes, cost, lat = L[pos[i]]
                t = self._model(eng, reads, writes, cost, lat)
                if best is None or t < best[0] - 1e-9:
                    best = (t, i)
            if best is None:
                break
            t, i = best
            kind, eng, fn, reads, writes, cost, lat = lists[i][pos[i]]
            pos[i] += 1
            if kind == "op":
                self.op(eng, fn, reads, writes, cost=cost, _t=t)
            else:
                self.dma(eng, fn, reads, writes, _t=t)

    def sb(self, shape, dt=F32, name=None):
        self.nt += 1
        return self.es.enter_context(self.nc.sbuf_tensor(name or f"sb{self.nt}", list(shape), dt))

    def ps(self, shape, dt=F32, name=None):
        self.nt += 1
        return self.es.enter_context(self.nc.psum_tensor(name or f"ps{self.nt}", list(shape), dt))

    def _deps(self, eng, reads, writes):
        evs = []
        for t in reads:
            b = t.b
            if b.last_w is not None:
                evs.append(b.last_w)
        for t in writes:
            b = t.b
            if b.last_w is not None:
                evs.append(b.last_w)
            evs.extend(b.reads)
        waits = {}
        seen = self.seen[eng]
        for (k, v) in evs:
            if k == "pe" and eng == "pe":
                continue
            if seen.get(k, 0) >= v:
                continue
            if waits.get(k, 0) < v:
                waits[k] = v
        for k, v in waits.items():
            seen[k] = v
        return list(waits.items())

    def _update(self, ev, reads, writes):
        wb = [t.b for t in writes]
        for b in wb:
            b.last_w = ev
            b.reads = []
        for t in reads:
            b = t.b
            if b not in wb:
                b.reads.append(ev)
                if len(b.reads) > 32:
                    d = {}
                    for (k, v) in b.reads:
                        if d.get(k, 0) < v:
                            d[k] = v
                    b.reads = list(d.items())

    def op(self, eng, fn, reads=(), writes=(), cost=0.4, _t=None):
        if self.capture is not None:
            self.capture.append(("op", eng, fn, reads, writes, cost, 0.15))
            return
        t = self._model(eng, reads, writes, cost, 0.15) if _t is None else _t
        self._commit(eng, reads, writes, cost, 0.15, t)
        waits = self._deps(eng, reads, writes)
        self.cnt[eng] += 1
        ev = (eng, self.cnt[eng])
        self.q[eng].append(("op", fn, waits, None))
        self._update(ev, reads, writes)

    def dma(self, eng, fns, reads, writes, _t=None):
        if self.capture is not None:
            self.capture.append(("dma", eng, fns, reads, writes, 0.1, 2.5))
            return
        t = self._model(eng, reads, writes, 0.1, 2.5) if _t is None else _t
        self._commit(eng, reads, writes, 0.1, 2.5, t)
        if not isinstance(fns, (list, tuple)):
            fns = [fns]
        waits = self._deps(eng, reads, writes)
        key = ("dma", getattr(writes[0], "semkey", None) or id(writes[0].b))
        self._keep = getattr(self, "_keep", [])
        self._keep.append(writes[0].b)
        self.dma_tot[key] = self.dma_tot.get(key, 0) + 16 * len(fns)
        ev = (key, self.dma_tot[key])
        first = True
        for fn in fns:
            self.q[eng].append(("dma", fn, waits if first else [], key))
            first = False
        self._update(ev, reads, writes)

    def barrier(self):
        for e in self.ENG:
            waits = []
            seen = self.seen[e]
            for k in self.ENG:
                if k != e and self.cnt[k] > seen.get(k, 0):
                    waits.append((k, self.cnt[k]))
                    seen[k] = self.cnt[k]
            for k, v in self.dma_tot.items():
                if v > seen.get(k, 0):
                    waits.append((k, v))
                    seen[k] = v
            if waits:
                self.q[e].append(("wait", None, waits, None))

    def emit(self):
        nc = self.nc
        sems = {}
        with ExitStack() as es2:
            for e in self.ENG:
                sems[e] = es2.enter_context(nc.semaphore(f"sem_{e}"))
            for i, k in enumerate(self.dma_tot.keys()):
                sems[k] = es2.enter_context(nc.semaphore(f"semd{i}"))
            block = es2.enter_context(nc.Block())

            def run(ename, engobj):
                for (kind, fn, waits, key) in self.q[ename]:
                    for (k, v) in waits:
                        engobj.wait_ge(sems[k], v)
                    if kind == "op":
                        fn(engobj).then_inc(sems[ename], 1)
                    elif kind == "dma":
                        fn(engobj).then_inc(sems[key], 16)

            @block.tensor
            def _(e):
                run("pe", e)

            @block.scalar
            def _(e):
                run("act", e)

            @block.vector
            def _(e):
                run("dve", e)

            @block.gpsimd
            def _(e):
                run("pool", e)

            @block.sync
            def _(e):
                run("sp", e)


class Arena:
    def __init__(self, S, nbytes):
        self.t = S.sb([128, nbytes // 4], F32, "arena")
        self.cap = nbytes
        self.off = 0

    def mark(self):
        return self.off

    def reset(self, m):
        self.off = m

    def alloc(self, free_shape, dt=F32, parts=128, name=""):
        n = int(np.prod(free_shape))
        sz = 2 if dt == BF16 else 4
        nb = (n * sz + 63) // 64 * 64
        assert self.off + nb <= self.cap, f"arena overflow allocating {name} {free_shape}: {self.off}+{nb}>{self.cap}"
        ap = self.t[0:parts, self.off // 4:(self.off + nb) // 4]
        if dt == BF16:
            ap = ap.bitcast(BF16)
        ap = ap[:, 0:n]
        if len(free_shape) == 2:
            ap = ap.rearrange("p (a b) -> p a b", a=free_shape[0])
        elif len(free_shape) == 3:
            ap = ap.rearrange("p (a b c) -> p a b c", a=free_shape[0], b=free_shape[1])
        elif len(free_shape) == 4:
            ap = ap.rearrange("p (a b c d) -> p a b c d", a=free_shape[0], b=free_shape[1], c=free_shape[2])
        self.off += nb
        return T(ap, name)


CM_OFF = {}


def _const_mats():
    idx = np.arange(128)
    s = idx[:, None]
    t = idx[None, :]
    same = (s // 64) == (t // 64)
    mats = {
        "ident": (s == t),
        "tri": same & (s <= t),
        "blk": same,
        "negtri": -1.0 * (same & (s <= t)),
        "striu": same & (s < t),
        "nstriu": -1.0 * (same & (s < t)),
        "nstril": -1.0 * (same & (s > t)),
        "ones": np.ones((128, 128)),
        "trifull": (s <= t),
    }
    cols = []
    off = 0
    for k, v in mats.items():
        CM_OFF[k] = off
        cols.append(np.asarray(v, np.float32))
        off += 128
    sel = np.zeros((128, 2), np.float32)
    sel[:64, 0] = 1
    sel[64:, 1] = 1
    CM_OFF["sel"] = off
    cols.append(sel)
    off += 2
    CM_OFF["nsel"] = off
    cols.append(-sel)
    off += 2
    return np.concatenate(cols, axis=1), off


CM_NP, CM_W = _const_mats()

RV = {}
_o = 0
for _n, _w in (("hng", 256), ("mu", 1056), ("w0", 256), ("a0", 256), ("kk", 256),
               ("ka", 256), ("rk", 256), ("lnw", 256), ("lnb", 256), ("v0", 256), ("lgam", 256)):
    RV[_n] = (_o, _w)
    _o += _w
RV_W = _o


def _rope_table(TP):
    half = 32
    inv_freq = (10000.0 ** (-np.arange(half, dtype=np.float32) / half)).astype(np.float32)
    pos = np.arange(TP, dtype=np.float32)
    ang = (pos[:, None] * inv_freq[None, :]).astype(np.float32)
    c = np.cos(ang).astype(np.float32)
    s = np.sin(ang).astype(np.float32)
    sc = np.float32(64 ** -0.5)
    tab = np.stack([np.stack([c, s], 1), np.stack([c * sc, s * sc], 1)], 1)
    return np.ascontiguousarray(tab.reshape(TP, 128)).astype(np.float32)


def _lgam_row():
    h = np.arange(4, dtype=np.float32)
    lg = np.log1p(-np.exp2(-5.0 - h)).astype(np.float32)
    return np.repeat(lg, 64).astype(np.float32)


def build(NT, NL=2, dbg=False):
    TP = NT * 128
    NB = 384 if NT % 3 == 0 else (256 if NT % 2 == 0 else 128)
    NBLK = TP // NB
    nc = bass.Bass("TRN2", target_bir_lowering=False)

    def din(name, shape, dt=F32):
        return nc.dram_tensor(name, list(shape), dt, kind="ExternalInput").ap()

    hT0 = din("hT0", [D, TP])
    w_in = din("w_in", [2, D, NIN])
    w_out = din("w_out", [2, D, D])
    w_up = din("w_up", [2, D, DFF])
    w_dn = din("w_dn", [2, DFF, D])
    gvec = din("gvec", [128, 40])
    rowv = din("rowv", [2, RV_W])
    lgts = din("lgts", [2, 256])
    s5vec = din("s5vec", [2, 128, 24])
    s5B = din("s5B", [2, 128, 2048])
    s5C = din("s5C", [2, 128, 2048])
    s5fm = din("s5fm", [2, 128, 4])
    glu_w = din("glu_w", [2, 256, 256])
    rw_wup = din("rw_wup", [2, 64, 256])
    rw_aup = din("rw_aup", [2, 64, 256])
    rw_gup = din("rw_gup", [2, 160, 256])
    rw_vdn = din("rw_vdn", [256, 32])
    rw_vup = din("rw_vup", [32, 256])
    cmat = din("cmat", [128, CM_W])
    rope = din("rope", [TP, 128])
    outT = nc.dram_tensor("outT", [D, TP], F32, kind="ExternalOutput").ap()
    hA = nc.dram_tensor("hA", [D, TP], F32, kind="Internal").ap()
    hB = nc.dram_tensor("hB", [D, TP], F32, kind="Internal").ap()
    vfirst = nc.dram_tensor("vfirst", [TP, 256], F32, kind="Internal").ap()
    zdram = nc.dram_tensor("zdram", [TP + 1, 3104], F32, kind="Internal").ap()
    udram = nc.dram_tensor("udram", [256, TP], F32, kind="Internal").ap()
    if dbg:
        dbg_o = nc.dram_tensor("dbg_o", [NL, D, TP], F32, kind="ExternalOutput").ap()
        dbg_h = nc.dram_tensor("dbg_h", [NL, D, TP], F32, kind="ExternalOutput").ap()

    with ExitStack() as es:
        S = Sched(nc, es)
        AR = Arena(S, 212000)
        PS = [T(S.ps([128, 512], F32, f"psb{i}"), f"ps{i}") for i in range(8)]

        def fsz(ap):
            n = 1
            for d_ in ap.shape[1:]:
                n *= int(d_)
            return n

        def ecost(eng, ap):
            n = fsz(ap)
            if eng == "pool":
                return 0.3 + 0.0028 * n
            return 0.2 + 0.00105 * n

        def mm(out, lhsT, rhs, R, W, start=True, stop=True):
            c_ = 0.03 + 0.00045 * max(64, fsz(rhs))
            if rhs.dtype == F32:
                c_ *= 4
            S.op("pe", lambda e: e.matmul(out, lhsT=lhsT, rhs=rhs, start=start, stop=stop), R, W, cost=c_)

        def tr(out, in_, ident, R, W):
            S.op("pe", lambda e: e.transpose(out=out, in_=in_, identity=ident), R, W, cost=0.07)

        def act(out, in_, func, R, W, bias=None, scale=None, eng="act"):
            kw = {}
            if bias is not None:
                kw["bias"] = bias
            if scale is not None:
                kw["scale"] = scale
            S.op(eng, lambda e: e.activation(out=out, in_=in_, func=func, **kw), R, W, cost=ecost(eng, out) + 0.15)

        def tt(out, in0, in1, op, R, W, eng="dve"):
            S.op(eng, lambda e: e.tensor_tensor(out=out, in0=in0, in1=in1, op=op), R, W, cost=ecost(eng, out))

        def ts(out, in0, s1, s2, op0, op1, R, W, eng="dve"):
            if s2 is None:
                S.op(eng, lambda e: e.tensor_single_scalar(out=out, in_=in0, scalar=s1, op=op0), R, W, cost=ecost(eng, out))
            else:
                S.op(eng, lambda e: e.tensor_scalar(out=out, in0=in0, scalar1=s1, scalar2=s2, op0=op0, op1=op1), R, W,
                     cost=ecost(eng, out))

        def stt(out, in0, scalar, in1, op0, op1, R, W, eng="dve"):
            S.op(eng, lambda e: e.scalar_tensor_tensor(out=out, in0=in0, scalar=scalar, in1=in1, op0=op0, op1=op1), R, W,
                 cost=ecost(eng, out))

        def cp(out, in_, R, W, eng="dve"):
            if eng == "act":
                S.op(eng, lambda e: e.activation(out=out, in_=in_, func=AF.Copy), R, W, cost=ecost(eng, out))
            else:
                S.op(eng, lambda e: e.tensor_copy(out=out, in_=in_), R, W, cost=ecost(eng, out))

        def recip(out, in_, R, W):
            S.op("dve", lambda e: e.reciprocal(out=out, in_=in_), R, W, cost=ecost("dve", out) + 0.1)

        def red(out, in_, R, W):
            S.op("dve", lambda e: e.tensor_reduce(out=out, in_=in_, axis=AX.X, op=ALU.add), R, W, cost=ecost("dve", in_))

        def memset(t, val, eng="pool"):
            S.op(eng, lambda e: e.memset(t.ap, val), [], [t])

        def dma(out, in_, R, W, eng="sp", **kw):
            S.dma(eng, lambda e: e.dma_start(out=out, in_=in_, **kw), R, W)

        CM = AR.alloc([CM_W], F32, name="cm")
        CMB = AR.alloc([256], BF16, name="cmb")
        GV = AR.alloc([40], F32, name="gvec")
        dma(CM.ap, cmat[:, :], [], [CM])
        dma(GV.ap, gvec[:, :], [], [GV])
        cp(CMB[:, 0:128], CM[:, CM_OFF["ident"]:CM_OFF["ident"] + 128], [CM], [CMB])
        cp(CMB[:, 128:256], CM[:, CM_OFF["trifull"]:CM_OFF["trifull"] + 128], [CM], [CMB])
        ZRO = AR.alloc([128], BF16, name="zeros")
        memset(ZRO, 0.0)
        for i_ in range(8):
            for q_ in range(4):
                S.op("pe", lambda e, i_=i_, q_=q_: e.matmul(PS[i_][:, q_ * 128:(q_ + 1) * 128], lhsT=ZRO[:, 0:128],
                                                            rhs=ZRO[:, 0:128], start=True, stop=True), [ZRO], [PS[i_]])

        def cm(name, w=128, bf=False, rows=128):
            if bf:
                o = {"ident": 0, "trifull": 128}[name]
                return CMB[0:rows, o:o + w]
            o = CM_OFF[name]
            return CM[0:rows, o:o + w]

        Hs = {}
        for m in ("ret", "hgrn", "rwkv"):
            Hs[m] = (AR.alloc([4, 64], F32, parts=64, name=f"H_{m}"), AR.alloc([4, 64], BF16, parts=64, name=f"Hb_{m}"))
        S5car = AR.alloc([16], F32, name="s5carry")
        ST5 = AR.alloc([16], F32, name="st5")
        pmark = AR.mark()

        hdram = {"src": None}
        DR = {"hA": T(hA, "hA"), "hB": T(hB, "hB"), "vf": T(vfirst, "vf"), "out": T(outT, "out")}
        ZD = [T(zdram, f"zd{i}", semkey=f"zd{i % 2}") for i in range(NT)]
        ZD0 = T(zdram, "zd0")
        UD = [T(udram, f"ud{i}", semkey=f"ud{i % 2}") for i in range(NT)]
        VF = [T(vfirst, f"vf{i}", semkey=f"vf{i % 2}") for i in range(NT)]
        if dbg:
            DR["dbg_o"] = T(dbg_o, "dbg_o")
            DR["dbg_h"] = T(dbg_h, "dbg_h")
        H0 = T(hT0, "hT0")

        def zproj_phase(l, src, src_ap):
            AR.reset(pmark)
            WIN = AR.alloc([8, NIN], BF16, name="win")
            for c in range(8):
                dma(WIN[:, c, :], w_in[l, c * 128:(c + 1) * 128, :], [], [WIN], eng="pool")
            HT2 = [AR.alloc([8, 128], F32, name=f"zp_ht{i}") for i in range(2)]
            HB2 = [AR.alloc([8, 128], BF16, name=f"zp_hb{i}") for i in range(2)]
            SQ2 = [AR.alloc([8, 128], F32, name=f"zp_sq{i}") for i in range(2)]
            ZT2 = [AR.alloc([3104], F32, name=f"zp_zt{i}") for i in range(2)]
            UT2 = [AR.alloc([2, 128], F32, name=f"zp_ut{i}") for i in range(2)]
            RS2 = [AR.alloc([1], F32, name=f"zp_rs{i}") for i in range(2)]
            RR2 = [AR.alloc([128], F32, name=f"zp_rr{i}") for i in range(2)]
            if l == 0:
                S.op("pool", lambda e: e.memset(ZT2[1][0:1, :], 0.0), [], [ZT2[1]])
                dma(zdram[0:1, :], ZT2[1][0:1, :], [ZT2[1]], [ZD0])
            g8 = GV[:, l * 8:(l + 1) * 8]
            chunks = [(n0, min(512, 3104 - n0)) for n0 in range(0, 3104, 512)]
            for it in range(NT):
                s_ = it % 2
                tsl = slice(it * 128, (it + 1) * 128)
                HT_, HB_, SQ_, ZT_, UT_, RS_, RR_ = HT2[s_], HB2[s_], SQ2[s_], ZT2[s_], UT2[s_], RS2[s_], RR2[s_]
                dma(HT_.ap, src_ap.rearrange("(c p) t -> p c t", p=128)[:, :, tsl], [src], [HT_])
                act(SQ_.ap, HT_.ap, AF.Square, [HT_], [SQ_])
                tt(HB_.ap, HT_.ap, g8.unsqueeze(2).broadcast_to([128, 8, 128]), ALU.mult, [HT_, GV], [HB_],
                   eng="pool" if s_ else "dve")
                pst = PS[6 + s_]
                for cch in range(8):
                    mm(pst[:, 0:128], cm("ones"), SQ_[:, cch, :], [CM, SQ_], [pst], start=(cch == 0), stop=(cch == 7))
                for cch in range(8):
                    mm(pst[:, 128:129], SQ_[:, cch, :], cm("ones", 1), [CM, SQ_], [pst], start=(cch == 0), stop=(cch == 7))
                act(RR_.ap, pst[:, 0:128], AF.Sqrt, [pst], [RR_], bias=EPS, scale=1.0 / D)
                recip(RR_.ap, RR_.ap, [RR_], [RR_])
                act(RS_.ap, pst[:, 128:129], AF.Sqrt, [pst], [RS_], bias=EPS, scale=1.0 / D)
                recip(RS_.ap, RS_.ap, [RS_], [RS_])
                for k, (n0, n) in enumerate(chunks):
                    pb = PS[k % 5]
                    for cch in range(8):
                        mm(pb[:, 0:n], HB_[:, cch, :], WIN[:, cch, n0:n0 + n], [HB_, WIN], [pb],
                           start=(cch == 0), stop=(cch == 7))
                    if k % 2 == 0:
                        act(ZT_[:, n0:n0 + n], pb[:, 0:n], AF.Copy, [pb, RS_], [ZT_], scale=RS_[:, 0:1])
                    else:
                        ts(ZT_[:, n0:n0 + n], pb[:, 0:n], RS_[:, 0:1], None, ALU.mult, None, [pb, RS_], [ZT_])
                pu_ = PS[5]
                for ct in range(2):
                    for cch in range(8):
                        mm(pu_[:, ct * 128:(ct + 1) * 128], WIN[:, cch, 3104 + ct * 128:3104 + (ct + 1) * 128], HB_[:, cch, :],
                           [WIN, HB_], [pu_], start=(cch == 0), stop=(cch == 7))
                tt(UT_.ap, pu_[:, 0:256].rearrange("p (a b) -> p a b", a=2),
                   RR_.ap.unsqueeze(1).broadcast_to([128, 2, 128]), ALU.mult, [pu_, RR_], [UT_])
                dma(zdram[1 + it * 128:1 + (it + 1) * 128, :], ZT_.ap, [ZT_], [ZD[it]])
                dma(udram.rearrange("(c p) t -> p c t", p=128)[:, :, tsl], UT_.ap, [UT_], [UD[it]])
            S.barrier()

        def mixer_phase(l, src, src_ap, dst, dst_ap):
            AR.reset(pmark)
            WOUT = AR.alloc([8, D], BF16, name="wout")
            for c in range(8):
                dma(WOUT[:, c, :], w_out[l, c * 128:(c + 1) * 128, :], [], [WOUT], eng="pool")
            GLW = AR.alloc([2, 256], BF16, name="glw")
            dma(GLW.ap, glu_w[l].rearrange("(c p) n -> p c n", p=128), [], [GLW], eng="pool")
            WUPr = AR.alloc([256], BF16, parts=64, name="rwwup")
            AUPr = AR.alloc([256], BF16, parts=64, name="rwaup")
            GUPr = AR.alloc([2, 256], BF16, name="rwgup")
            dma(WUPr.ap, rw_wup[l], [], [WUPr], eng="pool")
            dma(AUPr.ap, rw_aup[l], [], [AUPr], eng="pool")
            dma(GUPr[:, 0, :], rw_gup[l, 0:128, :], [], [GUPr], eng="pool")
            dma(GUPr[0:32, 1, :], rw_gup[l, 128:160, :], [], [GUPr], eng="pool")
            if l > 0:
                VDN = AR.alloc([2, 32], BF16, name="vdn")
                VUP = AR.alloc([256], BF16, parts=32, name="vup")
                dma(VDN.ap, rw_vdn.rearrange("(c p) n -> p c n", p=128), [], [VDN], eng="pool")
                dma(VUP.ap, rw_vup[:, :], [], [VUP], eng="pool")
            ROW = AR.alloc([RV_W], F32, name="rowv")
            dma(ROW.ap, rowv[l:l + 1, :].broadcast_to([128, RV_W]), [], [ROW])

            def row(name, lo=0, w=None):
                o, ww = RV[name]
                w = ww if w is None else w
                return ROW[:, o + lo:o + lo + w]

            LB = AR.alloc([256], F32, name="lb")
            OML = AR.alloc([256], F32, name="oml")
            dma(LB.ap, lgts[1:2, :].broadcast_to([128, 256]), [], [LB])
            dma(OML.ap, lgts[0:1, :].broadcast_to([128, 256]), [], [OML])
            tt(LB.ap, LB.ap, OML.ap, ALU.subtract, [LB, OML], [LB])
            act(LB.ap, LB.ap, AF.Sigmoid, [LB], [LB])
            ts(LB.ap, LB.ap, float(l), None, ALU.mult, None, [LB], [LB])
            ts(OML.ap, LB.ap, -1.0, 1.0, ALU.mult, ALU.add, [LB], [OML])

            S5V = AR.alloc([24], F32, name="s5v")
            S5F = AR.alloc([4], F32, name="s5fm")
            dma(S5V.ap, s5vec[l], [], [S5V])
            dma(S5F.ap, s5fm[l], [], [S5F])
            BBLK = AR.alloc([2048], BF16, name="bblk")
            dma(BBLK.ap, s5B[l], [], [BBLK], eng="pool")
            KC = AR.alloc([2048], BF16, name="kc")
            QT = AR.alloc([16, 128], BF16, name="qt")
            QL = AR.alloc([16], F32, name="ql")
            CB = AR.alloc([16, 128], BF16, name="cb")
            SV = AR.alloc([24, 8], F32, name="s5small")
            m5 = AR.mark()
            KCT = AR.alloc([16, 128], F32, name="kct")
            QTF = AR.alloc([16, 128], F32, name="qtf")
            CST = AR.alloc([2048], F32, name="cstage")
            TMPA = AR.alloc([8, 128], F32, name="tmpa")
            TMPB = AR.alloc([8, 128], F32, name="tmpb")
            dma(CST.ap, s5C[l], [], [CST])
            a_re, a_im, ldt = S5V[:, 0:8], S5V[:, 8:16], S5V[:, 16:24]
            sv = lambda i: SV[:, i, :]
            RW_ = [S5V, SV]
            act(sv(0), ldt, AF.Exp, RW_, [SV])
            tt(sv(1), a_re, sv(0), ALU.mult, RW_, [SV])
            tt(sv(2), a_im, sv(0), ALU.mult, RW_, [SV])
            act(sv(3), sv(1), AF.Exp, RW_, [SV])
            act(sv(4), sv(1), AF.Exp, RW_, [SV], scale=-1.0)
            act(sv(5), sv(2), AF.Sin, RW_, [SV], scale=1.0 / 16)
            act(sv(6), sv(2), AF.Sin, RW_, [SV], scale=1.0 / 8)
            tt(sv(7), sv(5), sv(5), ALU.mult, RW_, [SV])
            ts(sv(7), sv(7), -2.0, 1.0, ALU.mult, ALU.add, RW_, [SV])
            for _ in range(3):
                tt(sv(8), sv(7), sv(7), ALU.mult, RW_, [SV])
                tt(sv(9), sv(6), sv(6), ALU.mult, RW_, [SV])
                stt(sv(6), sv(7), 2.0, sv(6), ALU.mult, ALU.mult, RW_, [SV])
                tt(sv(7), sv(8), sv(9), ALU.subtract, RW_, [SV])
            tt(sv(10), sv(7), sv(3), ALU.mult, RW_, [SV])
            tt(sv(11), sv(6), sv(3), ALU.mult, RW_, [SV])
            tt(sv(12), sv(7), sv(4), ALU.mult, RW_, [SV])
            stt(sv(13), sv(6), -1.0, sv(4), ALU.mult, ALU.mult, RW_, [SV])
            ts(sv(16), sv(10), -1.0, None, ALU.add, None, RW_, [SV])
            tt(sv(17), a_re, a_re, ALU.mult, RW_, [SV])
            tt(sv(18), a_im, a_im, ALU.mult, RW_, [SV])
            tt(sv(17), sv(17), sv(18), ALU.add, RW_, [SV])
            recip(sv(17), sv(17), RW_, [SV])
            tt(sv(18), sv(16), a_re, ALU.mult, RW_, [SV])
            tt(sv(19), sv(11), a_im, ALU.mult, RW_, [SV])
            tt(sv(18), sv(18), sv(19), ALU.add, RW_, [SV])
            tt(sv(14), sv(18), sv(17), ALU.mult, RW_, [SV])
            tt(sv(18), sv(11), a_re, ALU.mult, RW_, [SV])
            tt(sv(19), sv(16), a_im, ALU.mult, RW_, [SV])
            tt(sv(18), sv(18), sv(19), ALU.subtract, RW_, [SV])
            tt(sv(15), sv(18), sv(17), ALU.mult, RW_, [SV])

            def powtab(TAB, sre, sim):
                memset_ap = TAB[:, 0:8, 0:1]
                S.op("dve", lambda e: e.memset(memset_ap, 1.0), [], [TAB])
                memset_ap2 = TAB[:, 8:16, 0:1]
                S.op("dve", lambda e: e.memset(memset_ap2, 0.0), [], [TAB])
                cp(sv(20), sv(sre), RW_, [SV])
                cp(sv(21), sv(sim), RW_, [SV])
                n = 1
                while n < 128:
                    pre = TAB[:, 0:8, 0:n]
                    pim = TAB[:, 8:16, 0:n]
                    bre = SV[:, 20, :].unsqueeze(2).broadcast_to([128, 8, n])
                    bim = SV[:, 21, :].unsqueeze(2).broadcast_to([128, 8, n])
                    t1 = TMPA[:, :, 0:n]
                    t2 = TMPB[:, :, 0:n]
                    tt(t1, pre, bre, ALU.mult, [TAB, SV], [TMPA])
                    tt(t2, pim, bim, ALU.mult, [TAB, SV], [TMPB])
                    tt(TAB[:, 0:8, n:2 * n], t1, t2, ALU.subtract, [TMPA, TMPB], [TAB])
                    tt(t1, pre, bim, ALU.mult, [TAB, SV], [TMPA])
                    tt(t2, pim, bre, ALU.mult, [TAB, SV], [TMPB])
                    tt(TAB[:, 8:16, n:2 * n], t1, t2, ALU.add, [TMPA, TMPB], [TAB])
                    tt(sv(22), sv(20), sv(20), ALU.mult, RW_, [SV])
                    tt(sv(23), sv(21), sv(21), ALU.mult, RW_, [SV])
                    stt(sv(21), sv(20), 2.0, sv(21), ALU.mult, ALU.mult, RW_, [SV])
                    tt(sv(20), sv(22), sv(23), ALU.subtract, RW_, [SV])
                    n *= 2

            powtab(QTF, 10, 11)
            cp(QT.ap, QTF.ap, [QTF], [QT], eng="act")
            cp(QL.ap, QTF[:, :, 127], [QTF], [QL], eng="dve")
            powtab(KCT, 12, 13)
            for j in range(16):
                pb = PS[j % 2]
                S.op("pe", lambda e, j=j, pb=pb: e.transpose(out=pb[:, 0:128], in_=KCT[:, j, :], identity=cm("ident")),
                     [KCT, CM], [pb])
                cp(KC[:, j * 128:(j + 1) * 128], pb[:, 0:128], [pb], [KC], eng="act" if j % 2 else "dve")
            zre = SV[:, 14, :].unsqueeze(2).broadcast_to([128, 8, 128])
            zim = SV[:, 15, :].unsqueeze(2).broadcast_to([128, 8, 128])
            cre = CST[:, 0:1024].rearrange("p (a b) -> p a b", a=8)
            cim = CST[:, 1024:2048].rearrange("p (a b) -> p a b", a=8)
            tt(TMPA.ap, cre, zre, ALU.mult, [CST, SV], [TMPA])
            tt(TMPB.ap, cim, zim, ALU.mult, [CST, SV], [TMPB])
            tt(CB[:, 0:8, :], TMPA.ap, TMPB.ap, ALU.subtract, [TMPA, TMPB], [CB])
            tt(TMPA.ap, cre, zim, ALU.mult, [CST, SV], [TMPA])
            tt(TMPB.ap, cim, zre, ALU.mult, [CST, SV], [TMPB])
            stt(CB[:, 8:16, :], TMPA.ap, -1.0, TMPB.ap, ALU.mult, ALU.subtract, [TMPA, TMPB], [CB])
            S.barrier()
            AR.reset(m5)

            HT = AR.alloc([8, 128], F32, name="hT")
            ROPE = AR.alloc([128], F32, name="rope")
            OT = AR.alloc([8, 128], BF16, name="oT")
            OTOK = AR.alloc([768], BF16, name="otok")
            OTr = T(OTOK[:, 0:256], "otr")
            OTh = T(OTOK[:, 256:512], "oth")
            OTw = T(OTOK[:, 512:768], "otw")
            OT5 = T(OT[:, 6:8, :], "ot5")
            OT6 = T(OT[:, 0:6, :], "ot6")
            identb = cm("ident", bf=True)

            class Ctx:
                pass

            def mkctx(tag, delta, banks):
                c = Ctx()
                c.delta = delta
                c.R = AR.alloc([256], F32, name=tag + "R")
                c.K = AR.alloc([256], F32, name=tag + "K")
                c.V = AR.alloc([256], BF16, name=tag + "V")
                c.LW = AR.alloc([256], F32, name=tag + "LW")
                c.CUM = AR.alloc([256], F32, name=tag + "CUM")
                c.E = AR.alloc([2, 256], F32, name=tag + "E")
                c.TOK = AR.alloc([8 if delta else 4, 256], BF16, name=tag + "TOK")
                c.XT = AR.alloc([4 if delta else 2, 4, 128], BF16, parts=64, name=tag + "XT")
                c.SC = AR.alloc([6 if delta else 1, 4, 128], BF16, name=tag + "SC")
                c.WC = AR.alloc([4, 2], F32, parts=64, name=tag + "WC")
                c.O = AR.alloc([256], F32, name=tag + "O")
                c.TA = AR.alloc([256], F32, name=tag + "TA")
                c.TB = AR.alloc([256], F32, name=tag + "TB")
                c.ST = AR.alloc([16], F32, name=tag + "ST")
                c.HM = AR.alloc([4, 64], BF16, parts=64, name=tag + "HM")
                c.XTM = AR.alloc([2 if delta else 1, 4, 2, 128], BF16, parts=64, name=tag + "XTM")
                memset(c.XTM, 0.0)
                if delta:
                    c.KK = AR.alloc([256], F32, name=tag + "KK")
                    c.BB = AR.alloc([256], F32, name=tag + "BB")
                    c.SC2 = AR.alloc([2, 4, 128], BF16, name=tag + "SC2")
                    c.P1 = AR.alloc([256], BF16, name=tag + "P1")
                    c.U = AR.alloc([256], BF16, name=tag + "U")
                    memset(c.U, 0.0)
                    X, Y, Z = [PS[i] for i in banks]
                    c.bk = dict(cum=X, wc=(Y, 0), tr=(X, Y), sc=(X, Y, X, Y, Z), pa=X, pb=Y, pr=Z, pp=X, pu=Y,
                                ph=(Z, 0), po=(Y, 256))
                    c.bk["cum"] = Z
                else:
                    X, Y = [PS[i] for i in banks]
                    c.bk = dict(cum=X, wc=(Y, 0), tr=(X, Y), sc=(X,), ph=(X, 0), po=(Y, 0))
                return c

            CR = mkctx("r", False, (0, 1))
            CR.seqp = CR
            CH = CR
            CW0 = mkctx("w", True, (4, 5, 6))
            import copy as _copy
            CW1 = _copy.copy(CW0)
            for nm_, shp_, dt_, pr_ in (("SC", [6, 4, 128], BF16, 128), ("TOK", [8, 256], BF16, 128),
                                        ("XTM", [2, 4, 2, 128], BF16, 64), ("V", [256], BF16, 128),
                                        ("WC", [4, 2], F32, 64)):
                setattr(CW1, nm_, AR.alloc(shp_, dt_, parts=pr_, name="w1" + nm_))
            memset(CW1.XTM, 0.0)
            SQP = Ctx()
            SQP.HM = CW0.HM
            SQP.P1 = CW0.P1
            SQP.U = CW0.U
            SQP.O = CW0.O
            SQP.TA = AR.alloc([256], F32, name="sqTA")
            SQP.TB = AR.alloc([256], F32, name="sqTB")
            SQP.ST = AR.alloc([16], F32, name="sqST")
            SQP.bk = dict(pp=(PS[2], 0), pu=(PS[2], 256), ph=(PS[3], 0), po=(PS[2], 0))
            CW0.seqp = SQP
            CW1.seqp = SQP
            CW = [CW0, CW1]
            for cw_ in CW:
                cw_.RWV = AR.alloc([256], F32, name="rwV")
                cw_.RWG = AR.alloc([256], F32, name="rwG")
                cw_.STb = AR.alloc([4], F32, name="stb")
            P5 = PS[7]
            ZBr = AR.alloc([1024], F32, name="zbr")
            ZBh = AR.alloc([1024], F32, name="zbh")
            S5A = AR.alloc([2048], F32, name="s5a")
            S5T1 = AR.alloc([512], F32, name="s5t1")
            S5T2 = AR.alloc([512], F32, name="s5t2")
            S5E = AR.alloc([2048], BF16, name="s5e")
            S5XB = T(S5E.ap.rearrange("p (a b) -> p a b", a=16), "s5xb", buf=S5E.b)
            S5XE = AR.alloc([16], F32, name="s5xend")
            UT = AR.alloc([2, 128], F32, name="uT")
            UTB = AR.alloc([2, 128], BF16, name="uTb")
            S5Y = AR.alloc([2, 128], F32, name="s5y")
            S5G = AR.alloc([2, 128], F32, name="s5g")
            S5YB = AR.alloc([2, 128], BF16, name="s5yb")
            ZR = AR.alloc([1056], F32, name="zrwkv")
            PREV = AR.alloc([1056], F32, name="zprev")
            TC = AR.alloc([256], F32, name="tC")
            TD = AR.alloc([256], F32, name="tD")
            TRB = AR.alloc([512], BF16, name="trb")
            TRT = AR.alloc([4, 128], BF16, name="trt")

            def zrows(it, shift=0):
                return slice(1 + it * 128 - shift, 1 + (it + 1) * 128 - shift)

            def load_ret(it):
                dma(ZBr.ap, zdram[zrows(it), 0:1024], [ZD[it]], [ZBr])
                dma(ROPE.ap, rope[it * 128:(it + 1) * 128, :], [], [ROPE])

            def load_hgrn(it):
                dma(ZBh.ap, zdram[zrows(it), 1024:2048], [ZD[it]], [ZBh])

            def load_zr(it):
                dma(ZR.ap, zdram[zrows(it), 2048:3104], [ZD[it]], [ZR])

            def load_prev(it):
                rd = [ZD[it]] + ([ZD[it - 1]] if it > 0 else [ZD0])
                dma(PREV.ap, zdram[zrows(it, 1), 2048:3104], rd, [PREV])

            def load_ut(it):
                dma(UT.ap, udram.rearrange("(c p) t -> p c t", p=128)[:, :, it * 128:(it + 1) * 128], [UD[it]], [UT])

            def gla_tile(c, mix, r_scale, stage="all"):
                delta = c.delta
                if stage in ("all", "prep"):
                    yield from gla_prep(c, mix, r_scale)
                if stage in ("all", "seq"):
                    yield from gla_seq(c, mix)

            def gla_prep(c, mix, r_scale):
                delta = c.delta
                H, Hb = Hs[mix]
                bk = c.bk
                gR, gK, gV, gLW, gCUM, gE, gTOK, gXT, gSC, gWC, gO, XTM = (c.R, c.K, c.V, c.LW, c.CUM, c.E, c.TOK, c.XT,
                                                                            c.SC, c.WC, c.O, c.XTM)
                pcum = bk["cum"]
                mm(pcum[:, 0:256], cm("tri"), gLW.ap, [CM, gLW], [pcum])
                mm(pcum[:, 256:512], cm("blk"), gLW.ap, [CM, gLW], [pcum])
                pwc, wco = bk["wc"]
                for h in range(4):
                    mm(pwc[0:64, wco + h * 2:wco + h * 2 + 2], gLW[:, h * 64:(h + 1) * 64], cm("sel", 2), [gLW, CM], [pwc])
                yield
                act(gWC.ap.rearrange("p a b -> p (a b)"), pwc[0:64, wco:wco + 8], AF.Exp, [pwc], [gWC])
                cp(gCUM.ap, pcum[:, 0:256], [pcum], [gCUM], eng="act")
                act(gE[:, 0, :], gCUM.ap, AF.Exp, [gCUM], [gE])
                act(gE[:, 1, :], gCUM.ap, AF.Exp, [gCUM], [gE], scale=-1.0)
                yield
                if r_scale != 1.0:
                    stt(gTOK[:, 0, :], gR.ap, r_scale, gE[:, 0, :], ALU.mult, ALU.mult, [gR, gE], [gTOK])
                else:
                    tt(gTOK[:, 0, :], gR.ap, gE[:, 0, :], ALU.mult, [gR, gE], [gTOK])
                tt(gTOK[:, 1, :], gK.ap, gE[:, 1, :], ALU.mult, [gK, gE], [gTOK])
                if delta:
                    gKK, gBB = c.KK, c.BB
                    tt(gTOK[:, 5, :], gBB.ap, gE[:, 1, :], ALU.mult, [gBB, gE], [gTOK])
                tt(gE[:, 0, :], pcum[:, 256:512], gCUM.ap, ALU.subtract, [pcum, gCUM], [gE])
                act(gE[:, 0, :], gE[:, 0, :], AF.Exp, [gE], [gE])
                yield
                stt(gTOK[:, 2, :], gK.ap, cm("sel", 1), gE[:, 0, :], ALU.mult, ALU.mult, [gK, gE, CM], [gTOK])
                stt(gTOK[:, 3, :], gK.ap, CM[:, CM_OFF["sel"] + 1:CM_OFF["sel"] + 2], gE[:, 0, :], ALU.mult, ALU.mult,
                    [gK, gE, CM], [gTOK])
                tlist = [0, 1]
                if delta:
                    stt(gTOK[:, 6, :], gBB.ap, cm("nsel", 1), gE[:, 0, :], ALU.mult, ALU.mult, [gBB, gE, CM], [gTOK])
                    stt(gTOK[:, 7, :], gBB.ap, CM[:, CM_OFF["nsel"] + 1:CM_OFF["nsel"] + 2], gE[:, 0, :], ALU.mult,
                        ALU.mult, [gBB, gE, CM], [gTOK])
                    tt(gE[:, 1, :], gCUM.ap, gLW.ap, ALU.subtract, [gCUM, gLW], [gE])
                    act(gE[:, 1, :], gE[:, 1, :], AF.Exp, [gE], [gE])
                    tt(gTOK[:, 4, :], gKK.ap, gE[:, 1, :], ALU.mult, [gKK, gE], [gTOK])
                    tlist = [0, 1, 4, 5]
                yield
                for xi, tix in enumerate(tlist):
                    pb = bk["tr"][xi % 2]
                    pbv = pb.ap.bitcast(BF16)
                    for h in range(4):
                        tr(pbv[0:64, h * 128:(h + 1) * 128], gTOK[:, tix, h * 64:(h + 1) * 64], identb, [gTOK, CMB], [pb])
                    cp(gXT[:, xi, :, :], pbv[0:64, 0:512].rearrange("p (a b) -> p a b", a=4), [pb], [gXT], eng="act")
                    yield
                for cc in range(2):
                    cp(XTM[:, 0, :, cc, cc * 64:(cc + 1) * 64], gXT[:, 0, :, cc * 64:(cc + 1) * 64], [gXT], [XTM], eng="pool")
                    if delta:
                        cp(XTM[:, 1, :, cc, cc * 64:(cc + 1) * 64], gXT[:, 2, :, cc * 64:(cc + 1) * 64], [gXT], [XTM], eng="pool")

                def scores(dst_ix, lt, rt, mask, pb):
                    for h in range(4):
                        mm(pb[:, h * 128:(h + 1) * 128], gXT[:, lt, h, :], gXT[:, rt, h, :], [gXT], [pb])
                    tt(gSC[:, dst_ix, :, :], pb.ap.rearrange("p (a b) -> p a b", a=4),
                       cm(mask).unsqueeze(1).broadcast_to([128, 4, 128]), ALU.mult, [pb, CM], [gSC])

                scores(0, 1, 0, "tri", bk["sc"][0])
                yield
                if delta:
                    gSC2 = c.SC2
                    scores(3, 3, 2, "nstriu", bk["sc"][1])
                    scores(4, 2, 3, "nstril", bk["sc"][2])
                    yield
                    scores(1, 3, 0, "negtri", bk["sc"][3])
                    scores(2, 1, 2, "striu", bk["sc"][4])
                    tt(gSC[:, 5, :, :], gSC[:, 3, :, :], identb.unsqueeze(1).broadcast_to([128, 4, 128]), ALU.add,
                       [gSC, CMB], [gSC])
                    yield
                    Pc, PTc = (gSC, 3), (gSC, 4)
                    pa, pbb, pr = bk["pa"], bk["pb"], bk["pr"]
                    for j in range(1, 6):
                        Pn, PTn = ((gSC2, 0), (gSC2, 1)) if j % 2 == 1 else ((gSC, 3), (gSC, 4))
                        for h in range(4):
                            mm(pbb[:, h * 128:(h + 1) * 128], Pc[0][:, Pc[1], h, :], PTc[0][:, PTc[1], h, :],
                               [PTc[0], Pc[0]], [pbb])
                        if j < 5:
                            for h in range(4):
                                mm(pa[:, h * 128:(h + 1) * 128], PTc[0][:, PTc[1], h, :], Pc[0][:, Pc[1], h, :],
                                   [PTc[0], Pc[0]], [pa])
                        yield
                        cp(PTn[0][:, PTn[1], :, :], pbb.ap.rearrange("p (a b) -> p a b", a=4), [pbb], [PTn[0]], eng="dve")
                        if j < 5:
                            cp(Pn[0][:, Pn[1], :, :], pa.ap.rearrange("p (a b) -> p a b", a=4), [pa], [Pn[0]], eng="act")
                        for h in range(4):
                            mm(pr[:, h * 128:(h + 1) * 128], PTn[0][:, PTn[1], h, :], gSC[:, 5, h, :],
                               [PTn[0], gSC], [pr])
                        yield
                        tt(gSC[:, 5, :, :], gSC[:, 5, :, :], pr.ap.rearrange("p (a b) -> p a b", a=4), ALU.add,
                           [gSC, pr], [gSC])
                        Pc, PTc = Pn, PTn

            def gla_seq(c, mix):
                delta = c.delta
                H, Hb = Hs[mix]
                q = c.seqp
                gV, gTOK, gSC, gWC, XTM, gO = c.V, c.TOK, c.SC, c.WC, c.XTM, q.O
                if delta:
                    gP1, Ub = q.P1, q.U
                    pp, ppo = q.bk["pp"]
                    pu, puo = q.bk["pu"]
                    ph, pho = q.bk["ph"]
                    po, poo = q.bk["po"]
                else:
                    ph, pho = c.bk["ph"]
                    po, poo = c.bk["po"]
                HM = q.HM
                hmid32 = q.TA[0:64, :].rearrange("p (a b) -> p a b", a=4)
                hnew32 = q.TB[0:64, :].rearrange("p (a b) -> p a b", a=4)
                for cc in range(2):
                    if delta:
                        for h in range(4):
                            o_ = pp[:, ppo + h * 64:ppo + (h + 1) * 64]
                            mm(o_, gSC[:, 2, h, :], gV[:, h * 64:(h + 1) * 64], [gSC, gV], [pp], start=True, stop=False)
                            mm(o_, XTM[:, 1, h, 0, :], Hb[:, h, :], [XTM, Hb], [pp], start=False, stop=(cc == 0))
                            if cc == 1:
                                mm(o_, XTM[:, 1, h, 1, :], HM[:, h, :], [XTM, HM], [pp], start=False, stop=True)
                        yield
                        cp(gP1.ap, pp[:, ppo:ppo + 256], [pp], [gP1], eng="act")
                        for h in range(4):
                            mm(pu[:, puo + h * 64:puo + (h + 1) * 64], gSC[:, 5, h, :], gP1[:, h * 64:(h + 1) * 64],
                               [gSC, gP1], [pu])
                        yield
                        cp(Ub.ap, pu[:, puo:puo + 256], [pu], [Ub], eng="dve")
                    for h in range(4):
                        o_ = ph[0:64, pho + h * 64:pho + (h + 1) * 64]
                        mm(o_, gTOK[:, 2 + cc, h * 64:(h + 1) * 64], gV[:, h * 64:(h + 1) * 64], [gTOK, gV], [ph],
                           start=True, stop=not delta)
                        if delta:
                            mm(o_, gTOK[:, 6 + cc, h * 64:(h + 1) * 64], Ub[:, h * 64:(h + 1) * 64], [gTOK, Ub], [ph],
                               start=False, stop=True)
                    wc = gWC[:, :, cc:cc + 1].broadcast_to([64, 4, 64])
                    ph3 = ph[0:64, pho:pho + 256].rearrange("p (a b) -> p a b", a=4)
                    yield
                    if cc == 0:
                        tt(hmid32, H.ap, wc, ALU.mult, [H, gWC], [q.TA])
                        tt(hmid32, hmid32, ph3, ALU.add, [q.TA, ph], [q.TA])
                        cp(HM.ap, hmid32, [q.TA], [HM], eng="act")
                    else:
                        tt(hnew32, hmid32, wc, ALU.mult, [q.TA, gWC], [q.TB])
                        for h in range(4):
                            o_ = po[:, poo + h * 64:poo + (h + 1) * 64]
                            mm(o_, gSC[:, 0, h, :], gV[:, h * 64:(h + 1) * 64], [gSC, gV], [po], start=True, stop=False)
                            if delta:
                                mm(o_, gSC[:, 1, h, :], Ub[:, h * 64:(h + 1) * 64], [gSC, Ub], [po], start=False, stop=False)
                            mm(o_, XTM[:, 0, h, 0, :], Hb[:, h, :], [XTM, Hb], [po], start=False, stop=False)
                            mm(o_, XTM[:, 0, h, 1, :], HM[:, h, :], [XTM, HM], [po], start=False, stop=True)
                        yield
                        cp(gO.ap, po[:, poo:poo + 256], [po], [gO], eng="act")
                        tt(H.ap, hnew32, ph3, ALU.add, [q.TB, ph], [H])
                        cp(Hb.ap, H.ap, [H], [Hb], eng="act")
                    yield

            def rms_heads_to(c, dst_ap, dstT, extra_row, gate_src, gateT):
                gO, TA, TB, ST = c.O, c.TA, c.TB, c.ST
                o3 = gO.ap.rearrange("p (a b) -> p a b", a=4)
                act(TA.ap, gO.ap, AF.Square, [gO], [TA])
                red(ST[:, 0:4], TA.ap.rearrange("p (a b) -> p a b", a=4), [TA], [ST])
                act(ST[:, 0:4], ST[:, 0:4], AF.Sqrt, [ST], [ST], bias=EPS, scale=1.0 / 64)
                recip(ST[:, 0:4], ST[:, 0:4], [ST], [ST])
                yield
                tt(TA.ap.rearrange("p (a b) -> p a b", a=4), o3, ST[:, 0:4].unsqueeze(2).broadcast_to([128, 4, 64]),
                   ALU.mult, [gO, ST], [TA])
                if extra_row is not None:
                    tt(TA.ap, TA.ap, extra_row, ALU.mult, [TA, ROW], [TA])
                act(TB.ap, gate_src, AF.Silu, [gateT], [TB])
                tt(dst_ap, TA.ap, TB.ap, ALU.mult, [TA, TB], [dstT])
                yield

            def thread_ret(it):
                c = CR
                ZB = ZBr
                gR, gK, gV, gLW, TA, TB = c.R, c.K, c.V, c.LW, c.TA, c.TB
                qk = ZB[:, 0:512].rearrange("p (a h c d) -> p a h c d", a=2, h=4, c=2)
                rp = ROPE.ap.rearrange("p (a c d) -> p a c d", a=2, c=2)
                outqk = [gR.ap.rearrange("p (h c d) -> p h c d", h=4, c=2), gK.ap.rearrange("p (h c d) -> p h c d", h=4, c=2)]
                for a in range(2):
                    src_ = qk[:, a]
                    cos4 = rp[:, a, 0, :].unsqueeze(1).broadcast_to([128, 4, 32])
                    sin4 = rp[:, a, 1, :].unsqueeze(1).broadcast_to([128, 4, 32])
                    t1v = src_[:, :, 0, :]
                    t2v = src_[:, :, 1, :]
                    ta = TA[:, 0:128].rearrange("p (h d) -> p h d", h=4)
                    tb = TB[:, 0:128].rearrange("p (h d) -> p h d", h=4)
                    o_ = outqk[a]
                    dT = gR if a == 0 else gK
                    tt(ta, t1v, cos4, ALU.mult, [ZB, ROPE], [TA], eng="pool")
                    tt(tb, t2v, sin4, ALU.mult, [ZB, ROPE], [TB], eng="pool")
                    tt(o_[:, :, 0, :], ta, tb, ALU.subtract, [TA, TB], [dT], eng="pool")
                    tt(ta, t1v, sin4, ALU.mult, [ZB, ROPE], [TA], eng="pool")
                    tt(tb, t2v, cos4, ALU.mult, [ZB, ROPE], [TB], eng="pool")
                    tt(o_[:, :, 1, :], ta, tb, ALU.add, [TA, TB], [dT], eng="pool")
                    yield
                cp(gV.ap, ZB[:, 512:768], [ZB], [gV], eng="act")
                cp(gLW.ap, row("lgam"), [ROW], [gLW], eng="pool")
                yield from gla_tile(c, "ret", 1.0)
                yield from rms_heads_to(c, OTOK[:, 0:256], OTr, None, ZB[:, 768:1024], ZB)
                if it + 1 < NT:
                    load_ret(it + 1)
                yield

            def thread_hgrn(it):
                c = CH
                ZB = ZBh
                gR, gK, gV, gLW, TA, TB = c.R, c.K, c.V, c.LW, c.TA, c.TB
                act(TA.ap, ZB[:, 256:512], AF.Sigmoid, [ZB], [TA])
                tt(TA.ap, TA.ap, OML.ap, ALU.mult, [TA, OML], [TA])
                tt(TA.ap, TA.ap, LB.ap, ALU.add, [TA, LB], [TA])
                yield
                act(gLW.ap, TA.ap, AF.Ln, [TA], [gLW])
                act(gK.ap, TA.ap, AF.Identity, [TA], [gK], scale=-1.0, bias=1.0)
                act(gR.ap, ZB[:, 0:256], AF.Silu, [ZB], [gR])
                cp(gV.ap, ZB[:, 512:768], [ZB], [gV], eng="act")
                yield
                yield from gla_tile(c, "hgrn", 0.125)
                yield from rms_heads_to(c, OTOK[:, 256:512], OTh, row("hng"), ZB[:, 768:1024], ZB)
                if it + 1 < NT:
                    load_hgrn(it + 1)
                yield

            def thread_s5(it):
                cp(UTB.ap, UT.ap, [UT], [UTB], eng="act")
                yield
                for n in range(4):
                    kc = n % 2
                    mm(P5[:, :], UTB[:, kc, :], BBLK[:, kc * 1024 + (n // 2) * 512:kc * 1024 + (n // 2) * 512 + 512],
                       [UTB, BBLK], [P5])
                    cp(S5A[:, n * 512:(n + 1) * 512], P5[:, :], [P5], [S5A], eng="act")
                    yield
                bre, bim = S5A[:, 0:1024], S5A[:, 1024:2048]
                kre, kim = KC[:, 0:1024], KC[:, 1024:2048]
                for hf in range(2):
                    hs = slice(hf * 512, (hf + 1) * 512)
                    hs2 = slice(1024 + hf * 512, 1024 + (hf + 1) * 512)
                    tt(S5T1.ap, kre[:, hs], bre[:, hs], ALU.mult, [KC, S5A], [S5T1])
                    tt(S5T2.ap, kim[:, hs], bim[:, hs], ALU.mult, [KC, S5A], [S5T2], eng="pool")
                    tt(S5E[:, hs], S5T1.ap, S5T2.ap, ALU.subtract, [S5T1, S5T2], [S5E])
                    yield
                    tt(S5T1.ap, kre[:, hs], bim[:, hs], ALU.mult, [KC, S5A], [S5T1])
                    tt(S5T2.ap, kim[:, hs], bre[:, hs], ALU.mult, [KC, S5A], [S5T2], eng="pool")
                    tt(S5E[:, hs2], S5T1.ap, S5T2.ap, ALU.add, [S5T1, S5T2], [S5E])
                    yield
                zc = S5A.ap.rearrange("p (a b) -> p a b", a=16)
                for q in range(4):
                    for j in range(q * 4, q * 4 + 4):
                        mm(P5[:, (j % 4) * 128:(j % 4 + 1) * 128], S5E[:, j * 128:(j + 1) * 128], cm("trifull", bf=True),
                           [S5E, CMB], [P5])
                    tt(zc[:, q * 4:(q + 1) * 4, :], P5.ap.rearrange("p (a b) -> p a b", a=4),
                       S5car[:, q * 4:(q + 1) * 4].unsqueeze(2).broadcast_to([128, 4, 128]), ALU.add,
                       [P5, S5car], [S5A])
                    yield
                tt(ST5[:, 0:8], QL[:, 0:8], zc[:, 0:8, 127], ALU.mult, [QL, S5A], [ST5])
                tt(ST5[:, 8:16], QL[:, 8:16], zc[:, 8:16, 127], ALU.mult, [QL, S5A], [ST5])
                tt(S5XE[:, 0:8], ST5[:, 0:8], ST5[:, 8:16], ALU.subtract, [ST5], [S5XE])
                tt(ST5[:, 0:8], QL[:, 0:8], zc[:, 8:16, 127], ALU.mult, [QL, S5A], [ST5])
                tt(ST5[:, 8:16], QL[:, 8:16], zc[:, 0:8, 127], ALU.mult, [QL, S5A], [ST5])
                tt(S5XE[:, 8:16], ST5[:, 0:8], ST5[:, 8:16], ALU.add, [ST5], [S5XE])
                yield
                t1 = S5T1.ap.rearrange("p (a b) -> p a b", a=4)
                t2 = S5T2.ap.rearrange("p (a b) -> p a b", a=4)
                for hf in range(2):
                    a0, a1 = hf * 4, hf * 4 + 4
                    tt(t1, QT[:, a0:a1, :], zc[:, a0:a1, :], ALU.mult, [QT, S5A], [S5T1])
                    tt(t2, QT[:, 8 + a0:8 + a1, :], zc[:, 8 + a0:8 + a1, :], ALU.mult, [QT, S5A], [S5T2], eng="pool")
                    tt(S5XB[:, a0:a1, :], t1, t2, ALU.subtract, [S5T1, S5T2], [S5XB])
                    yield
                    tt(t1, QT[:, a0:a1, :], zc[:, 8 + a0:8 + a1, :], ALU.mult, [QT, S5A], [S5T1])
                    tt(t2, QT[:, 8 + a0:8 + a1, :], zc[:, a0:a1, :], ALU.mult, [QT, S5A], [S5T2], eng="pool")
                    tt(S5XB[:, 8 + a0:8 + a1, :], t1, t2, ALU.add, [S5T1, S5T2], [S5XB])
                    yield
                tt(ST5[:, 0:8], S5XE[:, 0:8], SV[:, 10, :], ALU.mult, [S5XE, SV], [ST5])
                tt(ST5[:, 8:16], S5XE[:, 8:16], SV[:, 11, :], ALU.mult, [S5XE, SV], [ST5])
                tt(S5car[:, 0:8], ST5[:, 0:8], ST5[:, 8:16], ALU.subtract, [ST5], [S5car])
                tt(ST5[:, 0:8], S5XE[:, 0:8], SV[:, 11, :], ALU.mult, [S5XE, SV], [ST5])
                tt(ST5[:, 8:16], S5XE[:, 8:16], SV[:, 10, :], ALU.mult, [S5XE, SV], [ST5])
                tt(S5car[:, 8:16], ST5[:, 0:8], ST5[:, 8:16], ALU.add, [ST5], [S5car])
                yield
                py = P5
                for ct in range(2):
                    k = 0
                    for reim in range(2):
                        for jj in range(ct * 4, ct * 4 + 4):
                            j = reim * 8 + jj
                            mm(py[:, ct * 128:(ct + 1) * 128], CB[:, j, :], S5XB[:, j, :], [CB, S5XB], [py],
                               start=(k == 0), stop=(k == 7))
                            k += 1
                yield
                for ct in range(2):
                    stt(S5Y[:, ct, :], UT[:, ct, :], S5F[:, ct:ct + 1], py[:, ct * 128:(ct + 1) * 128], ALU.mult, ALU.add,
                        [UT, S5F, py], [S5Y])
                if it + 1 < NT:
                    load_ut(it + 1)
                act(S5Y.ap, S5Y.ap, AF.Gelu, [S5Y], [S5Y])
                cp(S5YB.ap, S5Y.ap, [S5Y], [S5YB], eng="act")
                pg = P5
                for nt_ in range(2):
                    for kc in range(2):
                        mm(pg[:, 256 + nt_ * 128:256 + (nt_ + 1) * 128], GLW[:, kc, nt_ * 128:(nt_ + 1) * 128], S5YB[:, kc, :],
                           [GLW, S5YB], [pg], start=(kc == 0), stop=(kc == 1))
                yield
                for nt_ in range(2):
                    act(S5G[:, nt_, :], pg[:, 256 + nt_ * 128:256 + (nt_ + 1) * 128], AF.Sigmoid, [pg, S5F], [S5G],
                        bias=S5F[:, 2 + nt_:3 + nt_])
                tt(OT[:, 6:8, :], S5Y.ap, S5G.ap, ALU.mult, [S5Y, S5G], [OT5])
                yield

            def rwkv_prep(it):
                tsl = slice(it * 128, (it + 1) * 128)
                c = CW[it % 2]
                RWG, RWV = c.RWG, c.RWV
                gR, gK, gV, gLW, gKK, gBB, TA, TB, ST = c.R, c.K, c.V, c.LW, c.KK, c.BB, c.TA, c.TB, c.ST
                tt(PREV.ap, PREV.ap, ZR.ap, ALU.subtract, [PREV, ZR], [PREV])
                tt(PREV.ap, PREV.ap, row("mu"), ALU.mult, [PREV, ROW], [PREV])
                tt(ZR.ap, ZR.ap, PREV.ap, ALU.add, [ZR, PREV], [ZR])
                if it + 1 < NT:
                    load_prev(it + 1)
                yield
                r_ap, k_ap, v_ap = ZR[:, 0:256], ZR[:, 256:512], ZR[:, 512:768]
                act(TRB[:, 0:64], ZR[:, 768:832], AF.Tanh, [ZR], [TRB])
                cp(TRB[:, 64:128], ZR[:, 832:896], [ZR], [TRB], eng="dve")
                act(TRB[:, 128:288], ZR[:, 896:1056], AF.Sigmoid, [ZR], [TRB])
                pb = PS[4]
                pbv = pb.ap.bitcast(BF16)
                tr(pbv[0:64, 0:128], TRB[:, 0:64], identb, [TRB, CMB], [pb])
                tr(pbv[0:64, 128:256], TRB[:, 64:128], identb, [TRB, CMB], [pb])
                tr(pbv[:, 256:384], TRB[:, 128:256], identb, [TRB, CMB], [pb])
                tr(pbv[0:32, 384:512], TRB[:, 256:288], identb, [TRB, CMB], [pb])
                yield
                cp(TRT[0:64, 0:2, :], pbv[0:64, 0:256].rearrange("p (a b) -> p a b", a=2), [pb], [TRT], eng="act")
                cp(TRT[:, 2, :], pbv[:, 256:384], [pb], [TRT], eng="act")
                cp(TRT[0:32, 3, :], pbv[0:32, 384:512], [pb], [TRT], eng="act")
                pw_ = PS[5]
                mm(pw_[:, 0:256], TRT[0:64, 0, :], WUPr.ap, [TRT, WUPr], [pw_])
                mm(pw_[:, 256:512], TRT[0:64, 1, :], AUPr.ap, [TRT, AUPr], [pw_])
                pg_ = PS[6]
                mm(pg_[:, 0:256], TRT[:, 2, :], GUPr[:, 0, :], [TRT, GUPr], [pg_], start=True, stop=False)
                mm(pg_[:, 0:256], TRT[0:32, 3, :], GUPr[0:32, 1, :], [TRT, GUPr], [pg_], start=False, stop=True)
                yield
                cp(RWG.ap, pg_[:, 0:256], [pg_], [RWG], eng="act")
                tt(TA.ap, pw_[:, 0:256], row("w0"), ALU.add, [pw_, ROW], [TA])
                act(TA.ap, TA.ap, AF.Sigmoid, [TA], [TA])
                ts(gLW.ap, TA.ap, -float(np.exp(-0.5)), None, ALU.mult, None, [TA], [gLW])
                tt(TB.ap, pw_[:, 256:512], row("a0"), ALU.add, [pw_, ROW], [TB])
                act(TB.ap, TB.ap, AF.Sigmoid, [TB], [TB])
                yield
                if l == 0:
                    cp(RWV.ap, v_ap, [ZR], [RWV], eng="pool")
                    dma(vfirst[tsl, :], RWV.ap, [RWV], [VF[it]])
                else:
                    cp(TRB[:, 0:256], v_ap, [ZR], [TRB], eng="act")
                    pb = PS[4]
                    pbv = pb.ap.bitcast(BF16)
                    tr(pbv[:, 0:128], TRB[:, 0:128], identb, [TRB, CMB], [pb])
                    tr(pbv[:, 128:256], TRB[:, 128:256], identb, [TRB, CMB], [pb])
                    yield
                    cp(TRT[:, 0:2, :], pbv[:, 0:256].rearrange("p (a b) -> p a b", a=2), [pb], [TRT], eng="act")
                    pv = PS[5]
                    mm(pv[0:32, 0:128], VDN[:, 0, :], TRT[:, 0, :], [VDN, TRT], [pv], start=True, stop=False)
                    mm(pv[0:32, 0:128], VDN[:, 1, :], TRT[:, 1, :], [VDN, TRT], [pv], start=False, stop=True)
                    yield
                    cp(TRB[0:32, 256:384], pv[0:32, 0:128], [pv], [TRB], eng="act")
                    pv2 = PS[6]
                    mm(pv2[:, 0:256], TRB[0:32, 256:384], VUP.ap, [TRB, VUP], [pv2])
                    yield
                    tt(TC.ap, pv2[:, 0:256], row("v0"), ALU.add, [pv2, ROW], [TC])
                    act(TC.ap, TC.ap, AF.Sigmoid, [TC], [TC])
                    dma(TD.ap, vfirst[tsl, :], [VF[it]], [TD])
                    tt(TD.ap, TD.ap, v_ap, ALU.subtract, [TD, ZR], [TD])
                    tt(TD.ap, TD.ap, TC.ap, ALU.mult, [TD, TC], [TD])
                    tt(RWV.ap, TD.ap, v_ap, ALU.add, [TD, ZR], [RWV])
                cp(gV.ap, RWV.ap, [RWV], [gV], eng="act")
                yield
                tt(gKK.ap, k_ap, row("kk"), ALU.mult, [ZR, ROW], [gKK])
                act(TC.ap, gKK.ap, AF.Square, [gKK], [TC])
                red(ST[:, 0:4], TC.ap.rearrange("p (a b) -> p a b", a=4), [TC], [ST])
                act(ST[:, 0:4], ST[:, 0:4], AF.Sqrt, [ST], [ST])
                ts(ST[:, 0:4], ST[:, 0:4], 1e-12, None, ALU.max, None, [ST], [ST])
                recip(ST[:, 0:4], ST[:, 0:4], [ST], [ST])
                yield
                tt(gKK.ap.rearrange("p (a b) -> p a b", a=4), gKK.ap.rearrange("p (a b) -> p a b", a=4),
                   ST[:, 0:4].unsqueeze(2).broadcast_to([128, 4, 64]), ALU.mult, [gKK, ST], [gKK])
                tt(gBB.ap, gKK.ap, TB.ap, ALU.mult, [gKK, TB], [gBB])
                stt(TC.ap, TB.ap, -1.0, row("ka"), ALU.add, ALU.mult, [TB, ROW], [TC])
                stt(gK.ap, TC.ap, 1.0, k_ap, ALU.add, ALU.mult, [TC, ZR], [gK])
                cp(gR.ap, r_ap, [ZR], [gR], eng="pool")
                yield
                tt(TD.ap, gR.ap, gK.ap, ALU.mult, [gR, gK], [TD])
                tt(TD.ap, TD.ap, row("rk"), ALU.mult, [TD, ROW], [TD])
                red(c.STb[:, 0:4], TD.ap.rearrange("p (a b) -> p a b", a=4), [TD], [c.STb])
                if it + 1 < NT:
                    load_zr(it + 1)
                yield
                yield from gla_tile(c, "rwkv", 1.0, stage="prep")

            def rwkv_seq(it):
                c = CW[it % 2]
                q = c.seqp
                RWG, RWV = c.RWG, c.RWV
                TA, TC, ST = q.TA, q.TB, q.ST
                yield from gla_tile(c, "rwkv", 1.0, stage="seq")
                gO = q.O
                o3 = gO.ap.rearrange("p (a b) -> p a b", a=4)
                red(ST[:, 8:12], o3, [gO], [ST])
                ts(ST[:, 8:12], ST[:, 8:12], 1.0 / 64, None, ALU.mult, None, [ST], [ST])
                tt(TA.ap.rearrange("p (a b) -> p a b", a=4), o3, ST[:, 8:12].unsqueeze(2).broadcast_to([128, 4, 64]),
                   ALU.subtract, [gO, ST], [TA])
                act(TC.ap, TA.ap, AF.Square, [TA], [TC])
                red(ST[:, 12:16], TC.ap.rearrange("p (a b) -> p a b", a=4), [TC], [ST])
                yield
                act(ST[:, 12:16], ST[:, 12:16], AF.Sqrt, [ST], [ST], bias=64e-5, scale=1.0 / 64)
                recip(ST[:, 12:16], ST[:, 12:16], [ST], [ST])
                tt(TA.ap.rearrange("p (a b) -> p a b", a=4), TA.ap.rearrange("p (a b) -> p a b", a=4),
                   ST[:, 12:16].unsqueeze(2).broadcast_to([128, 4, 64]), ALU.mult, [TA, ST], [TA])
                yield
                tt(TA.ap, TA.ap, row("lnw"), ALU.mult, [TA, ROW], [TA])
                tt(TA.ap, TA.ap, row("lnb"), ALU.add, [TA, ROW], [TA])
                tt(TC.ap.rearrange("p (a b) -> p a b", a=4), RWV.ap.rearrange("p (a b) -> p a b", a=4),
                   c.STb[:, 0:4].unsqueeze(2).broadcast_to([128, 4, 64]), ALU.mult, [RWV, c.STb], [TC])
                tt(TA.ap, TA.ap, TC.ap, ALU.add, [TA, TC], [TA])
                tt(OTOK[:, 512:768], TA.ap, RWG.ap, ALU.mult, [TA, RWG], [OTw])
                yield

            for m in ("ret", "hgrn", "rwkv"):
                memset(Hs[m][0], 0.0)
                memset(Hs[m][1], 0.0)
            memset(S5car, 0.0)
            load_zr(0)
            load_prev(0)
            for _ in rwkv_prep(0):
                pass
            load_ret(0)
            load_hgrn(0)
            load_ut(0)

            for it in range(NT):
                tsl = slice(it * 128, (it + 1) * 128)
                dma(HT.ap, src_ap.rearrange("(c p) t -> p c t", p=128)[:, :, tsl], [src], [HT])
                import os
                _th = os.environ.get("KTH", "rwkv,s5,hgrn,ret").split(",")
                def thread_rh(it_):
                    yield from thread_ret(it_)
                    yield from thread_hgrn(it_)
                gens = [rwkv_seq(it)]
                if it + 1 < NT:
                    gens.append(rwkv_prep(it + 1))
                gens += [thread_s5(it), thread_ret(it), thread_hgrn(it)] if False else [thread_s5(it), thread_rh(it)]
                S.run_threads(gens)

                pb = PS[0]
                pbv = pb.ap.bitcast(BF16)
                for j in range(6):
                    tr(pbv[:, j * 128:(j + 1) * 128], OTOK[:, j * 128:(j + 1) * 128], identb, [OTr, OTh, OTw, CMB], [pb])
                cp(OT[:, 0:6, :], pbv[:, 0:768].rearrange("p (a b) -> p a b", a=6), [pb], [OT6], eng="act")
                if dbg:
                    cp(S5A[:, 0:1024], OT.ap.rearrange("p a b -> p (a b)"), [OT5, OT6], [S5A], eng="dve")
                    dma(dbg_o[l].rearrange("(c p) t -> p c t", p=128)[:, :, tsl],
                        S5A[:, 0:1024].rearrange("p (a b) -> p a b", a=8), [S5A], [DR["dbg_o"]])
                for half in range(2):
                    pb = PS[2 + half]
                    for q in range(4):
                        cidx = half * 4 + q
                        for cch in range(8):
                            mm(pb[:, q * 128:(q + 1) * 128], WOUT[:, cch, cidx * 128:(cidx + 1) * 128], OT[:, cch, :],
                               [WOUT, OT5, OT6], [pb], start=(cch == 0), stop=(cch == 7))
                    tt(HT[:, half * 4:(half + 1) * 4, :], HT[:, half * 4:(half + 1) * 4, :],
                       pb.ap.rearrange("p (a b) -> p a b", a=4), ALU.add, [HT, pb], [HT])
                dma(dst_ap.rearrange("(c p) t -> p c t", p=128)[:, :, tsl], HT.ap, [HT], [dst])
                if dbg:
                    dma(dbg_h[l].rearrange("(c p) t -> p c t", p=128)[:, :, tsl], HT.ap, [HT], [DR["dbg_h"]])
            S.barrier()

        def ffn_phase(l, src, src_ap, dst, dst_ap, final):
            AR.reset(pmark)
            WUP = AR.alloc([8, DFF], BF16, name="wup")
            WDN = AR.alloc([32, D], BF16, name="wdn")
            for c in range(8):
                dma(WUP[:, c, :], w_up[l, c * 128:(c + 1) * 128, :], [], [WUP], eng="pool")
            for c in range(32):
                dma(WDN[:, c, :], w_dn[l, c * 128:(c + 1) * 128, :], [], [WDN], eng="pool")
            HT = AR.alloc([8, NB], F32, name="f_hT")
            HB = AR.alloc([8, NB], BF16, name="f_hb")
            SQ = AR.alloc([8, NB], F32, name="f_sq")
            RS = AR.alloc([NB], F32, name="f_rstd")
            HID = AR.alloc([32, NB], BF16, name="f_hid")
            TMP = [T(SQ[:, i, :], f"f_tmp{i}") for i in range(2)]
            gcol = 16 + l * 8
            for ib in range(NBLK):
                bsl = slice(ib * NB, (ib + 1) * NB)
                dma(HT.ap, src_ap.rearrange("(c p) t -> p c t", p=128)[:, :, bsl], [src], [HT])
                act(SQ.ap, HT.ap, AF.Square, [HT], [SQ, TMP[0], TMP[1]])
                tt(HB.ap, HT.ap, GV[:, gcol:gcol + 8].unsqueeze(2).broadcast_to([128, 8, NB]), ALU.mult, [HT, GV], [HB])
                pst = PS[0]
                for c in range(8):
                    mm(pst[:, 0:NB], cm("ones"), SQ[:, c, :], [CM, SQ], [pst], start=(c == 0), stop=(c == 7))
                act(RS.ap, pst[:, 0:NB], AF.Sqrt, [pst], [RS], bias=EPS, scale=1.0 / D)
                recip(RS.ap, RS.ap, [RS], [RS])
                for f in range(32):
                    pb = PS[1 + f % 3]
                    for c in range(8):
                        mm(pb[:, 0:NB], WUP[:, c, f * 128:(f + 1) * 128], HB[:, c, :], [WUP, HB], [pb],
                           start=(c == 0), stop=(c == 7))
                    tm = TMP[f % 2]
                    stt(tm.ap, pb[:, 0:NB], 0.0, RS.ap, ALU.max, ALU.mult, [pb, RS, SQ], [tm])
                    act(HID[:, f, :], tm.ap, AF.Square, [tm], [HID])
                for dt_ in range(8):
                    pb = PS[4 + dt_ % 3]
                    for f in range(32):
                        mm(pb[:, 0:NB], WDN[:, f, dt_ * 128:(dt_ + 1) * 128], HID[:, f, :], [WDN, HID], [pb],
                           start=(f == 0), stop=(f == 31))
                    tt(HT[:, dt_, :], HT[:, dt_, :], pb[:, 0:NB], ALU.add, [HT, pb], [HT])
                if not final:
                    dma(dst_ap.rearrange("(c p) t -> p c t", p=128)[:, :, bsl], HT.ap, [HT], [dst])
                else:
                    act(SQ.ap, HT.ap, AF.Square, [HT], [SQ, TMP[0], TMP[1]])
                    for c in range(8):
                        mm(pst[:, 0:NB], cm("ones"), SQ[:, c, :], [CM, SQ], [pst], start=(c == 0), stop=(c == 7))
                    act(RS.ap, pst[:, 0:NB], AF.Sqrt, [pst], [RS], bias=EPS, scale=1.0 / D)
                    recip(RS.ap, RS.ap, [RS], [RS])
                    tt(SQ.ap, HT.ap, GV[:, 32:40].unsqueeze(2).broadcast_to([128, 8, NB]), ALU.mult, [HT, GV], [SQ])
                    tt(SQ.ap, SQ.ap, RS.ap.unsqueeze(1).broadcast_to([128, 8, NB]), ALU.mult, [SQ, RS], [SQ])
                    dma(dst_ap.rearrange("(c p) t -> p c t", p=128)[:, :, bsl], SQ.ap, [SQ], [dst])
            S.barrier()

        cur, cur_ap = H0, hT0
        for l in range(NL):
            zproj_phase(l, cur, cur_ap)
            import os
            if os.environ.get("KSTOP") == "z":
                break
            mixer_phase(l, cur, cur_ap, DR["hA"], hA)
            if os.environ.get("KSTOP") == "m":
                break
            last = (l == NL - 1)
            if last:
                ffn_phase(l, DR["hA"], hA, DR["out"], outT, True)
            else:
                ffn_phase(l, DR["hA"], hA, DR["hB"], hB, False)
                cur, cur_ap = DR["hB"], hB
        S.barrier()
        S.emit()
    return nc


def host_inputs(inp, NT, b):
    TP = NT * 128
    f = np.float32
    x = np.asarray(inp["x"], f)
    h0 = np.zeros((TP, D), f)
    h0[:NMETA] = np.asarray(inp["meta_tokens"], f)
    nreal = min(SEQ, TP - NMETA)
    h0[NMETA:NMETA + nreal] = x[b, :nreal]
    m = {}
    m["hT0"] = np.ascontiguousarray(h0.T)
    m["w_in"] = np.asarray(inp["w_in"], f)
    m["w_out"] = np.asarray(inp["w_out"], f)
    m["w_up"] = np.asarray(inp["w_ffn_up"], f)
    m["w_dn"] = np.asarray(inp["w_ffn_down"], f)
    fm = lambda v: np.asarray(v, f).reshape(8, 128).T
    m["gvec"] = np.ascontiguousarray(np.concatenate(
        [fm(inp["norm_mix_g"][0]), fm(inp["norm_mix_g"][1]), fm(inp["norm_ffn_g"][0]), fm(inp["norm_ffn_g"][1]),
         fm(inp["norm_f_g"])], axis=1)).astype(f)
    rv = np.zeros((2, RV_W), f)
    for l in range(2):
        def put(name, v):
            o, w = RV[name]
            rv[l, o:o + w] = np.asarray(v, f).reshape(-1)
        put("hng", np.tile(np.asarray(inp["hgrn_norm_g"][l], f), 4))
        put("mu", inp["rwkv_mu"][l])
        put("w0", inp["rwkv_w0"][l])
        put("a0", inp["rwkv_a0"][l])
        put("kk", inp["rwkv_k_k"][l])
        put("ka", inp["rwkv_k_a"][l])
        put("rk", inp["rwkv_r_k"][l])
        put("lnw", inp["rwkv_ln_w"][l])
        put("lnb", inp["rwkv_ln_b"][l])
        if l > 0:
            put("v0", inp["rwkv_v0"][l - 1])
        put("lgam", _lgam_row())
    m["rowv"] = rv
    m["lgts"] = np.asarray(inp["hgrn_lb_logits"], f)
    def st(v):
        return np.asarray(v, f).reshape(8, 128).T
    s5vec = np.zeros((2, 128, 24), f)
    s5B = np.zeros((2, 128, 2, 2, 8, 64), f)
    s5C = np.zeros((2, 128, 2, 8, 8, 16), f)
    s5fm = np.zeros((2, 128, 4), f)
    for l in range(2):
        s5vec[l, :, 0:8] = st(inp["s5_a_re"][l])
        s5vec[l, :, 8:16] = st(inp["s5_a_im"][l])
        s5vec[l, :, 16:24] = st(np.repeat(np.asarray(inp["s5_log_dt"][l], f)[:, None], 64, axis=1))
        for ri, key in enumerate(("s5_b_re", "s5_b_im")):
            bb = np.asarray(inp[key][l], f)
            for g in range(16):
                kc, g8 = g // 8, g % 8
                s5B[l, g8 * 16:(g8 + 1) * 16, kc, ri, g8, :] = bb[g].T
        for ri, key in enumerate(("s5_c_re", "s5_c_im")):
            cc = np.asarray(inp[key][l], f)
            for g in range(16):
                jj, gl = g // 2, g % 2
                g8 = g % 8
                s5C[l, gl * 64:(gl + 1) * 64, ri, jj, g8, :] = cc[g].T
        s5fm[l, :, 0:2] = np.asarray(inp["s5_d"][l], f).reshape(2, 128).T
        s5fm[l, :, 2:4] = np.asarray(inp["s5_glu_b"][l], f).reshape(2, 128).T
    m["s5vec"] = s5vec
    m["s5B"] = s5B.reshape(2, 128, 2048)
    m["s5C"] = s5C.reshape(2, 128, 2048)
    m["s5fm"] = s5fm
    m["glu_w"] = np.asarray(inp["s5_glu_w"], f)
    m["rw_wup"] = np.asarray(inp["rwkv_w_up"], f)
    m["rw_aup"] = np.asarray(inp["rwkv_a_up"], f)
    m["rw_gup"] = np.asarray(inp["rwkv_g_up"], f)
    m["rw_vdn"] = np.asarray(inp["rwkv_v_down"][0], f)
    m["rw_vup"] = np.asarray(inp["rwkv_v_up"][0], f)
    m["cmat"] = CM_NP
    m["rope"] = _rope_table(TP)
    return m


_NC_CACHE = {}


def kernel(**inputs):
    NT = 33
    if NT not in _NC_CACHE:
        _NC_CACHE[NT] = build(NT)
    nc = _NC_CACHE[NT]
    shared = host_inputs(inputs, NT, 0)
    in_maps = []
    for b in range(8):
        mb = dict(shared)
        if b > 0:
            mb["hT0"] = host_inputs_h(inputs, NT, b)
        in_maps.append(mb)
    res = run_bass_kernel_spmd(nc, in_maps, core_ids=list(range(8)))
    out = np.empty((8, SEQ, D), np.float32)
    for b in range(8):
        out[b] = res.results[b]["outT"][:, NMETA:NMETA + SEQ].T
    return out


def host_inputs_h(inp, NT, b):
    TP = NT * 128
    h0 = np.zeros((TP, D), np.float32)
    h0[:NMETA] = np.asarray(inp["meta_tokens"], np.float32)
    h0[NMETA:NMETA + SEQ] = np.asarray(inp["x"], np.float32)[b]
    return np.ascontiguousarray(h0.T)
```
